# Optimizing a Trainium2 kernel written in Bass

```python
import jax, jax.numpy as jnp
from jax import lax
import numpy as np

D_MODEL = 1024
BATCH = 8
SEQ = 4096
DEPTH = 4

CHUNK = 64
Q_BLOCK = 128

D_MIX = D_MODEL
FOX_HEADS = 8
FOX_HEAD_DIM = 64
FOX_W = FOX_HEADS * FOX_HEAD_DIM
MLA_HEADS = 8
MLA_NOPE = 64
MLA_ROPE = 32
MLA_V = 64
MLA_W = MLA_HEADS * MLA_V
Q_LORA = 256
KV_LORA = 128
ROPE_THETA = 10000.0
EPS = 1e-6

IN_SIZES = (FOX_W, FOX_W, FOX_W, FOX_HEADS, FOX_W,
            Q_LORA, KV_LORA, MLA_ROPE, MLA_W)
N_IN = FOX_W * 4 + FOX_HEADS + Q_LORA + KV_LORA + MLA_ROPE + MLA_W

kernel_name = "hybrid_fox_mla_adaln_trunk"


def rms_norm(x, g):
    xf = x.astype(jnp.float32)
    y = xf * lax.rsqrt(jnp.mean(xf * xf, axis=-1, keepdims=True) + EPS) * g.astype(jnp.float32)
    return y.astype(x.dtype)


def to_blocks(a):
    b, s = a.shape[0], a.shape[1]
    return a.reshape((b, s // Q_BLOCK, Q_BLOCK) + a.shape[2:]).swapaxes(0, 1)


def from_blocks(a):
    nb, b, qb = a.shape[0], a.shape[1], a.shape[2]
    return a.swapaxes(0, 1).reshape((b, nb * qb) + a.shape[3:])


def apply_rope(t, cos, sin):
    tf = t.astype(jnp.float32)
    t1, t2 = jnp.split(tf, 2, axis=-1)
    out = jnp.concatenate([t1 * cos - t2 * sin, t2 * cos + t1 * sin], axis=-1)
    return out.astype(t.dtype)


def fox_attention(q, k, v, log_f):
    s_len = q.shape[1]
    cum = jnp.cumsum(log_f, axis=1)
    cum_k = cum.transpose(0, 2, 1)
    key_idx = jnp.arange(s_len)
    scale = FOX_HEAD_DIM ** -0.5

    def block(args):
        qb, cum_qb, start = args
        s = jnp.einsum('bqhd,bkhd->bhqk', qb, k, preferred_element_type=jnp.float32) * scale
        s = s + cum_qb.transpose(0, 2, 1)[..., None] - cum_k[:, :, None, :]
        q_idx = start + jnp.arange(Q_BLOCK)
        mask = key_idx[None, :] <= q_idx[:, None]
        p = jax.nn.softmax(jnp.where(mask, s, -jnp.inf), axis=-1)
        return jnp.einsum('bhqk,bkhd->bqhd', p.astype(v.dtype), v)

    starts = jnp.arange(s_len // Q_BLOCK, dtype=jnp.int32) * Q_BLOCK
    out = lax.map(block, (to_blocks(q), to_blocks(cum), starts))
    return from_blocks(out)


def mla_attention(q_nope, q_rope, k_nope, k_rope, v):
    s_len = q_nope.shape[1]
    key_chunk = jnp.arange(s_len) // CHUNK
    scale = (MLA_NOPE + MLA_ROPE) ** -0.5

    def block(args):
        qn, qr, start = args
        s = jnp.einsum('bqhd,bkhd->bhqk', qn, k_nope, preferred_element_type=jnp.float32)
        s = s + jnp.einsum('bqhr,bkr->bhqk', qr, k_rope, preferred_element_type=jnp.float32)
        q_chunk = (start + jnp.arange(Q_BLOCK)) // CHUNK
        mask = key_chunk[None, :] <= q_chunk[:, None]
        p = jax.nn.softmax(jnp.where(mask, s * scale, -jnp.inf), axis=-1)
        return jnp.einsum('bhqk,bkhd->bqhd', p.astype(v.dtype), v)

    starts = jnp.arange(s_len // Q_BLOCK, dtype=jnp.int32) * Q_BLOCK
    out = lax.map(block, (to_blocks(q_nope), to_blocks(q_rope), starts))
    return from_blocks(out)


def setup_inputs(seed: int = 0) -> dict:
    key = jax.random.key(seed)
    ks = jax.random.split(key, 16)
    f32 = jnp.float32
    x = jax.random.normal(ks[0], (BATCH, SEQ, D_MODEL), f32)
    c = jax.random.normal(ks[1], (BATCH, D_MODEL), f32)
    offset = jax.random.randint(ks[2], (BATCH,), 0, 16, dtype=jnp.int32) * CHUNK
    positions = (offset[:, None] + jnp.arange(SEQ, dtype=jnp.int32)[None, :]).astype(jnp.int32)
    norm_g = 1.0 + 0.02 * jax.random.normal(ks[3], (DEPTH, D_MODEL), f32)
    w_ada = 0.5 * jax.random.normal(ks[4], (DEPTH, D_MODEL, 3 * D_MODEL), f32) * D_MODEL ** -0.5
    b_ada = 0.02 * jax.random.normal(ks[5], (DEPTH, 3 * D_MODEL), f32)
    w_in = jax.random.normal(ks[6], (DEPTH, D_MODEL, N_IN), f32) * D_MODEL ** -0.5
    b_f = jax.random.uniform(ks[7], (DEPTH, FOX_HEADS), f32, 1.0, 4.0)
    q_norm_g = 1.0 + 0.02 * jax.random.normal(ks[8], (DEPTH, Q_LORA), f32)
    w_uq = jax.random.normal(ks[9], (DEPTH, Q_LORA, MLA_HEADS * (MLA_NOPE + MLA_ROPE)), f32) * Q_LORA ** -0.5
    kv_norm_g = 1.0 + 0.02 * jax.random.normal(ks[10], (DEPTH, KV_LORA), f32)
    w_ukv = jax.random.normal(ks[11], (DEPTH, KV_LORA, MLA_HEADS * (MLA_NOPE + MLA_V)), f32) * KV_LORA ** -0.5
    w_out = jax.random.normal(ks[12], (DEPTH, D_MIX, D_MODEL), f32) * D_MIX ** -0.5
    final_g = 1.0 + 0.02 * jax.random.normal(ks[13], (D_MODEL,), f32)
    return {"x": x, "c": c, "positions": positions, "norm_g": norm_g, "w_ada": w_ada,
            "b_ada": b_ada, "w_in": w_in, "b_f": b_f, "q_norm_g": q_norm_g, "w_uq": w_uq,
            "kv_norm_g": kv_norm_g, "w_ukv": w_ukv, "w_out": w_out, "final_g": final_g}


def reference(x, c, positions, norm_g, w_ada, b_ada, w_in, b_f, q_norm_g, w_uq,
              kv_norm_g, w_ukv, w_out, final_g):
    b, s_len, _ = x.shape
    splits = [int(i) for i in np.cumsum(IN_SIZES)[:-1]]

    inv_freq = 1.0 / (ROPE_THETA ** (jnp.arange(0, MLA_ROPE, 2, dtype=jnp.float32) / MLA_ROPE))
    ang = positions.astype(jnp.float32)[..., None] * inv_freq
    cos, sin = jnp.cos(ang), jnp.sin(ang)

    c_act = jax.nn.silu(c)
    for l in range(DEPTH):
        mod = c_act @ w_ada[l] + b_ada[l]
        shift, scale, gate = jnp.split(mod, 3, axis=-1)
        h = rms_norm(x, norm_g[l]) * (1.0 + scale[:, None, :]) + shift[:, None, :]

        z = h @ w_in[l]
        fq, fk, fv, ff, fg, q_lat, kv_lat, k_r, mg = jnp.split(z, splits, axis=-1)

        log_f = jax.nn.log_sigmoid(ff.astype(jnp.float32) + b_f[l].astype(jnp.float32))
        y_fox = fox_attention(fq.reshape(b, s_len, FOX_HEADS, FOX_HEAD_DIM),
                              fk.reshape(b, s_len, FOX_HEADS, FOX_HEAD_DIM),
                              fv.reshape(b, s_len, FOX_HEADS, FOX_HEAD_DIM), log_f)
        y_fox = y_fox.reshape(b, s_len, FOX_W) * jax.nn.silu(fg)

        q = (rms_norm(q_lat, q_norm_g[l]) @ w_uq[l]).reshape(b, s_len, MLA_HEADS, MLA_NOPE + MLA_ROPE)
        q_nope, q_rope = q[..., :MLA_NOPE], q[..., MLA_NOPE:]
        q_rope = apply_rope(q_rope, cos[:, :, None, :], sin[:, :, None, :])
        kv = (rms_norm(kv_lat, kv_norm_g[l]) @ w_ukv[l]).reshape(b, s_len, MLA_HEADS, MLA_NOPE + MLA_V)
        k_nope, v = kv[..., :MLA_NOPE], kv[..., MLA_NOPE:]
        k_rope = apply_rope(k_r, cos, sin)
        y_mla = mla_attention(q_nope, q_rope, k_nope, k_rope, v)
        y_mla = y_mla.reshape(b, s_len, MLA_W) * jax.nn.silu(mg)

        y = jnp.concatenate([y_fox, y_mla], axis=-1) @ w_out[l]
        x = x + gate[:, None, :] * y

    return rms_norm(x, final_g)
```

```python
import numpy as np
from contextlib import ExitStack
import concourse.bass as bass
import concourse.mybir as mybir
from concourse.bass_utils import run_bass_kernel_spmd

F32 = mybir.dt.float32
BF16 = mybir.dt.bfloat16
I32 = mybir.dt.int32
AF = mybir.ActivationFunctionType
ALU = mybir.AluOpType

D = 1024
S = 4096
NB = 8
BLK = 512
EPS = 1e-6
MLA_SCALE = float(96 ** -0.5)
W_FOX = 4096
W_MLA = 2112
W_COM = 4160
TWO_PI = float(2 * np.pi)
C1 = 6.28125
C2 = float(2 * np.pi - 6.28125)

def _col_layout(nl):
    off = {}
    o = 0
    for name, n in (("ng", nl * 8), ("fg", 8), ("qg", nl * 2), ("kvg", nl), ("bf", nl),
                    ("invf", 1), ("scl", 1), ("cc", 8)):
        off[name] = o
        o += n
    off["_n"] = o
    return off

CB_ID, CB_MF, CB_MM, CB_SQ, CB_SK, CB_SR, CB_N = 0, 128, 256, 384, 664, 944, 1040


class _Op:
    __slots__ = ("eng", "fn", "deps", "idx", "needed", "dma", "sem", "val", "prev")


class Sched:
    ENGS = ("pe", "act", "dve", "pool", "sp")

    def __init__(self):
        self.streams = {e: [] for e in self.ENGS}
        self.keys = {}
        self.seeds = {}

    def _state(self, key):
        st = self.keys.get(key)
        if st is None:
            st = [list(self.seeds.get(key[0], [])), []]
            self.keys[key] = st
        return st

    @staticmethod
    def _reduce(deps, op):
        best = {}
        dmas = {}
        for d in deps:
            if d is op:
                continue
            if d.dma:
                dmas[id(d)] = d
            else:
                b = best.get(d.eng)
                if b is None or d.idx > b.idx:
                    best[d.eng] = d
        return list(best.values()) + list(dmas.values())

    def add(self, eng, fn, reads=(), writes=(), dma=False):
        op = _Op()
        op.eng, op.fn, op.dma, op.needed = eng, fn, dma, False
        op.sem = op.val = op.prev = None
        op.idx = len(self.streams[eng])
        deps = []
        for k in reads:
            deps.extend(self._state(k)[0])
        for k in writes:
            st = self._state(k)
            deps.extend(st[0])
            deps.extend(st[1])
        op.deps = self._reduce(deps, op)
        for k in reads:
            st = self._state(k)
            if op.dma:
                st[1].append(op)
            else:
                st[1] = [r for r in st[1] if r.dma or r.eng != eng] + [op]
        for k in writes:
            st = self._state(k)
            st[0] = [op]
            st[1] = []
        self.streams[eng].append(op)
        return op

    def alias(self, from_names, to_names):
        deps = []
        for k in list(self.keys):
            if k[0] in from_names:
                st = self.keys.pop(k)
                deps.extend(st[0])
                deps.extend(st[1])
        for n in from_names:
            deps.extend(self.seeds.pop(n, []))
        deps = self._reduce(deps, None)
        for n in to_names:
            self.seeds[n] = deps

    @staticmethod
    def _needs_wait(op, d):
        if d.dma:
            return True
        if d.eng == op.eng:
            if op.eng == "pe":
                return False
            if op.idx - d.idx > 3:
                return False
        return True

    def finalize(self, sem_eng, dma_sems):
        for ops in self.streams.values():
            for op in ops:
                if op.dma:
                    op.needed = True
                for d in op.deps:
                    if self._needs_wait(op, d):
                        d.needed = True
        for e, ops in self.streams.items():
            cnt = 0
            k = 0
            pool = dma_sems.get(e, [])
            hist = []
            for op in ops:
                if not op.needed:
                    continue
                if op.dma:
                    n = len(pool)
                    op.sem = pool[k % n]
                    op.val = 16 * (k // n + 1)
                    op.prev = hist[k - n] if k >= n else None
                    hist.append(op)
                    k += 1
                else:
                    cnt += 1
                    op.sem = sem_eng[e]
                    op.val = cnt

    def emit(self, eng, E):
        seen = {}
        seen_dma = set()
        nwait = 0
        for op in self.streams[eng]:
            for d in op.deps:
                if not self._needs_wait(op, d):
                    continue
                if d.dma:
                    if id(d) in seen_dma:
                        continue
                    E.wait_ge(d.sem, d.val)
                    seen_dma.add(id(d))
                    nwait += 1
                else:
                    if seen.get(d.eng, 0) >= d.val:
                        continue
                    E.wait_ge(d.sem, d.val)
                    seen[d.eng] = d.val
                    nwait += 1
            if op.dma and op.needed and op.prev is not None and id(op.prev) not in seen_dma:
                E.wait_ge(op.prev.sem, op.prev.val)
                seen_dma.add(id(op.prev))
                nwait += 1
            if op.fn is None:
                continue
            inst = op.fn(E)
            if op.needed:
                inst.then_inc(op.sem, 16 if op.dma else 1)
        return nwait


def build_program(nl, last):
    nc = bass.Bass("TRN2", target_bir_lowering=False)
    CL = _col_layout(nl)

    def din(name, shape, dt=F32):
        return nc.dram_tensor(name, shape, dt, kind="ExternalInput").ap()

    xT_in = din("xT", [D, S])
    pos_d = din("pos", [1, S], I32)
    cols_d = din("cols", [128, CL["_n"]])
    cbf_d = din("cbf", [128, CB_N])
    wfox_d = din("wfox", [nl, 4, 128, W_FOX])
    wmla_d = din("wmla", [nl, 4, 128, W_MLA])
    wcom_d = din("wcom", [nl, 128, W_COM])
    wout_d = din("wout", [nl, 128, 8192])
    wada_d = din("wada", [nl, 6, 128, 4096])
    bada_d = din("bada", [nl, 6, 1, 512])
    out_d = nc.dram_tensor("out", [D, S], F32, kind="ExternalOutput").ap()
    xs_d = nc.dram_tensor("xs_scr", [D, S], F32, kind="Internal").ap()
    ysc_d = nc.dram_tensor("ysc_scr", [8, 128, S], BF16, kind="Internal").ap()
    rC_d = nc.dram_tensor("ropec_scr", [96, S], F32, kind="Internal").ap()
    rS_d = nc.dram_tensor("ropes_scr", [96, S], F32, kind="Internal").ap()

    sch = Sched()
    A = sch.add
    import os as _os
    _kstop = _os.environ.get("KSTOP", "")
    _nopf = _os.environ.get("K_NOPF", "") == "1"
    _ordrr = True

    class _Stop(Exception):
        pass

    def stop_at(tag):
        if _kstop == tag:
            raise _Stop()

    with ExitStack() as es:
        def sb(name, shape, dt):
            return es.enter_context(nc.sbuf_tensor(name, shape, dt))

        hT_t = sb("hT", [128, 8 * S], BF16)
        hT = hT_t[:].rearrange("p (k t) -> p k t", t=S)
        R1 = sb("R1", [128, 22528], BF16)
        R2 = sb("R2", [128, 16384], BF16)
        WP = sb("WP", [128, W_FOX], BF16)
        WC = sb("WC", [128, W_COM], BF16)
        CBF = sb("CBF", [128, CB_N], BF16)
        COLS = sb("COLS", [128, CL["_n"]], F32)
        MODT = sb("MODT", [128, nl * 24], F32)
        GP1 = sb("GP1", [128, nl * 8], F32)
        NBF = sb("NBF", [128, nl], F32)
        CSIL = sb("CSIL", [128, 8], F32)
        ONESB = sb("ONESB", [128, 128], BF16)
        ONE1 = sb("ONE1", [1, 1], F32)
        NT = 7
        T = [sb(f"T{i}", [128, 512], F32) for i in range(NT)]
        SQ = [sb(f"SQ{i}", [128, 512], BF16) for i in range(2)]
        HB = SQ
        PT = [sb(f"PT{i}", [128, 1024], BF16) for i in range(3)]
        OT = [[sb(f"OT{i}{h}", [128, 512], F32) for h in range(2)] for i in range(2)]
        LM = [sb(f"LM{i}", [128, 512], F32) for i in range(2)]
        EE = [sb(f"EE{i}", [128, 512], F32) for i in range(2)]
        GG = [sb(f"GG{i}", [128, 512], BF16) for i in range(2)]
        YS = [sb(f"YS{i}", [128, 512], BF16) for i in range(2)]
        RC = sb("RC", [128, 512], F32)
        RS = sb("RS", [128, 512], F32)
        PSA = [es.enter_context(nc.psum_tensor(f"psA{i}", [128, 1024], F32)) for i in range(2)]
        PSS = [es.enter_context(nc.psum_tensor(f"ps{i}", [128, 512], F32)) for i in range(4, 8)]
        PS = [PSA[i // 2][:, (i % 2) * 512:(i % 2 + 1) * 512] for i in range(4)] + [t[:] for t in PSS]

        QT = [R1[:, i * S:(i + 1) * S] for i in range(2)]
        KT = [R1[:, 2 * S + i * S: 2 * S + (i + 1) * S] for i in range(2)]
        VV = R1[:, 4 * S: 4 * S + 32 * 192].rearrange("p (t c) -> p t c", c=192)
        NXO = 12
        XOS = [R1[:, i * 1024:(i + 1) * 1024].bitcast(F32) for i in range(NXO)]
        YRB = [R1[:, 12288 + i * 4096: 12288 + (i + 1) * 4096].rearrange("p (c t) -> p c t", t=512)
               for i in range(2)]
        QN = R2[:, 0:8192].rearrange("p (k t) -> p k t", t=S)
        KVN = R2[:, 8192:12288]
        KR = R2[:, 12288:16384]
        WOUT = R2[:, 0:8192].rearrange("p (k c) -> p k c", c=1024)
        CC = R2[:, 8192:12288]
        WST = [R2[:, i * 8192:(i + 1) * 8192].bitcast(F32).rearrange("p (k c) -> p k c", c=512)
               for i in range(2)]
        ident = CBF[:, CB_ID:CB_ID + 128]
        maskF = CBF[:, CB_MF:CB_MF + 128]
        maskM = CBF[:, CB_MM:CB_MM + 128]
        selq = [CBF[0:72, CB_SQ + p * 70: CB_SQ + (p + 1) * 70] for p in range(4)]
        selk = [CBF[0:72, CB_SK + p * 70: CB_SK + (p + 1) * 70] for p in range(4)]
        selr = CBF[0:32, CB_SR:CB_SR + 96]

        def col(name, i=0, rows=128):
            o = CL[name] + i
            return COLS[0:rows, o:o + 1]

        MB, MB2, GB = 6, 7, 7
        PBS = [0, 1, 2, 3, 4, 5]
        state = {"pb": 0, "epi": 0}

        def next_pb():
            state["pb"] = (state["pb"] + 1) % len(PBS)
            return PBS[state["pb"]]

        def mm(out, lhsT, rhs, start, stop, reads, writes, skip=False):
            A("pe", lambda e: e.matmul(out, lhsT=lhsT, rhs=rhs, start=start, stop=stop,
                                       skip_group_check=skip), reads, writes)

        def act(out, in_, func, reads, writes, bias=None, scale=None):
            kw = {}
            if bias is not None:
                kw["bias"] = bias
            if scale is not None:
                kw["scale"] = scale
            A("act", lambda e: e.activation(out=out, in_=in_, func=func, **kw), reads, writes)

        def ts(eng, out, in0, s1, s2, op0, op1, reads, writes):
            if op1 is None:
                A(eng, lambda e: e.tensor_scalar(out=out, in0=in0, scalar1=s1, scalar2=None, op0=op0),
                  reads, writes)
            else:
                A(eng, lambda e: e.tensor_scalar(out=out, in0=in0, scalar1=s1, scalar2=s2, op0=op0, op1=op1),
                  reads, writes)

        def tt(eng, out, in0, in1, op, reads, writes):
            A(eng, lambda e: e.tensor_tensor(out=out, in0=in0, in1=in1, op=op), reads, writes)

        def stt(out, in0, scalar, in1, op0, op1, reads, writes):
            A("dve", lambda e: e.scalar_tensor_tensor(out=out, in0=in0, scalar=scalar, in1=in1, op0=op0, op1=op1),
              reads, writes)

        def cp(eng, out, in_, reads, writes):
            A(eng, lambda e: e.tensor_copy(out=out, in_=in_), reads, writes)

        def mset(eng, ap, val, writes):
            A(eng, lambda e: e.memset(ap, val), (), writes)

        def dma(out, in_, reads, writes, q="sp", **kw):
            A(q, lambda e: e.dma_start(out=out, in_=in_, **kw), reads, writes, dma=True)

        def bslice(b):
            return slice(b * BLK, (b + 1) * BLK)

        dma(CBF[:], cbf_d, (), [("cbf",)], q="pool")
        dma(COLS[:], cols_d, (), [("cols",)])
        mset("dve", ONESB[:], 1.0, [("onesb",)])
        mset("dve", ONE1[:], 1.0, [("one1",)])
        ts("dve", NBF[:], COLS[:, CL["bf"]:CL["bf"] + nl], -1.0, None, ALU.mult, None, [("cols",)], [("nbf",)])
        act(CSIL[:], COLS[:, CL["cc"]:CL["cc"] + 8], AF.Exp, [("cols",)], [("csil",)], scale=-1.0)
        ts("dve", CSIL[:], CSIL[:], 1.0, None, ALU.add, None, [("csil",)], [("csil",)])
        A("dve", lambda e: e.reciprocal(out=CSIL[:], in_=CSIL[:]), [("csil",)], [("csil",)])
        tt("dve", CSIL[:], CSIL[:], COLS[:, CL["cc"]:CL["cc"] + 8], ALU.mult, [("csil",), ("cols",)], [("csil",)])

        ada_cnt = [0]

        def ada_unit(l, cb):
            wi = ada_cnt[0] % 2
            ada_cnt[0] += 1
            dma(WST[wi], wada_d[l, cb].rearrange("p (k c) -> p k c", c=512), (), [("wst", wi)])
            dma(T[0][0:1, :], bada_d[l, cb], (), [("T", 0)])
            for k in range(8):
                mm(PS[MB2][0:1, :], CSIL[:, k:k + 1], WST[wi][:, k, :], k == 0, False,
                   [("csil",), ("wst", wi)], [("ps", MB2)])
            mm(PS[MB2][0:1, :], ONE1[0:1, 0:1], T[0][0:1, :], False, True, [("one1",), ("T", 0)], [("ps", MB2)])
            act(T[1][0:1, :], PS[MB2][0:1, :], AF.Copy, [("ps", MB2)], [("T", 1)])
            pbk = next_pb()
            for j in range(4):
                mm(PS[pbk][:, j:j + 1], T[1][0:1, j * 128:(j + 1) * 128], ONE1[0:1, 0:1], True, True,
                   [("T", 1), ("one1",)], [("ps", pbk)], skip=True)
            act(MODT[:, l * 24 + cb * 4: l * 24 + cb * 4 + 4], PS[pbk][:, 0:4], AF.Copy,
                [("ps", pbk)], [("modT",)])

        def ada_finish(l):
            stt(GP1[:, l * 8:(l + 1) * 8], MODT[:, l * 24 + 8: l * 24 + 16], 1.0,
                COLS[:, CL["ng"] + l * 8: CL["ng"] + (l + 1) * 8], ALU.add, ALU.mult,
                [("modT",), ("cols",)], [("gp1",)])

        def shift_col(l, d):
            return MODT[:, l * 24 + d: l * 24 + d + 1]

        def gate_col(l, d):
            return MODT[:, l * 24 + 16 + d: l * 24 + 16 + d + 1]

        def gp1_col(l, d):
            return GP1[:, l * 8 + d: l * 8 + d + 1]

        for b in range(NB):
            PI_ = T[2][0:96, :].bitcast(I32)
            dma(PI_, pos_d[0:1, bslice(b)].broadcast_to([96, BLK]), (), [("T", 2)])
            cp("dve", T[3][0:96, :], PI_, [("T", 2)], [("T", 3)])
            ts("dve", T[3][0:96, :], T[3][0:96, :], col("invf", 0, 96), None, ALU.mult, None,
               [("T", 3), ("cols",)], [("T", 3)])
            ts("dve", T[4][0:96, :], T[3][0:96, :], float(1.0 / TWO_PI), None, ALU.mult, None,
               [("T", 3)], [("T", 4)])
            KI_ = T[2][0:96, :].bitcast(I32)
            cp("dve", KI_, T[4][0:96, :], [("T", 4)], [("T", 2)])
            cp("dve", T[4][0:96, :], KI_, [("T", 2)], [("T", 4)])
            stt(T[3][0:96, :], T[4][0:96, :], -C1, T[3][0:96, :], ALU.mult, ALU.add,
                [("T", 4), ("T", 3)], [("T", 3)])
            stt(T[3][0:96, :], T[4][0:96, :], -C2, T[3][0:96, :], ALU.mult, ALU.add,
                [("T", 4), ("T", 3)], [("T", 3)])

            def wrap(tr, tm):
                ts("dve", T[tm][0:96, :], T[tr][0:96, :], float(np.pi), -TWO_PI, ALU.is_gt, ALU.mult,
                   [("T", tr)], [("T", tm)])
                tt("dve", T[tr][0:96, :], T[tr][0:96, :], T[tm][0:96, :], ALU.add,
                   [("T", tr), ("T", tm)], [("T", tr)])
                ts("dve", T[tm][0:96, :], T[tr][0:96, :], -float(np.pi), TWO_PI, ALU.is_lt, ALU.mult,
                   [("T", tr)], [("T", tm)])
                tt("dve", T[tr][0:96, :], T[tr][0:96, :], T[tm][0:96, :], ALU.add,
                   [("T", tr), ("T", tm)], [("T", tr)])

            wrap(3, 4)
            ts("dve", T[5][0:96, :], T[3][0:96, :], float(np.pi / 2), None, ALU.add, None,
               [("T", 3)], [("T", 5)])
            wrap(5, 4)
            ts("dve", T[3][0:96, :], T[3][0:96, :], 3.141592, -3.141592, ALU.min, ALU.max, [("T", 3)], [("T", 3)])
            ts("dve", T[5][0:96, :], T[5][0:96, :], 3.141592, -3.141592, ALU.min, ALU.max, [("T", 5)], [("T", 5)])
            act(T[3][0:96, :], T[3][0:96, :], AF.Sin, [("T", 3)], [("T", 3)])
            act(T[5][0:96, :], T[5][0:96, :], AF.Sin, [("T", 5)], [("T", 5)])
            ts("dve", T[3][0:96, :], T[3][0:96, :], col("scl", 0, 96), None, ALU.mult, None,
               [("T", 3), ("cols",)], [("T", 3)])
            ts("dve", T[5][0:96, :], T[5][0:96, :], col("scl", 0, 96), None, ALU.mult, None,
               [("T", 5), ("cols",)], [("T", 5)])
            dma(rS_d[:, bslice(b)], T[3][0:96, :], [("T", 3)], [("ropeS", b)])
            dma(rC_d[:, bslice(b)], T[5][0:96, :], [("T", 5)], [("ropeC", b)])
            if b < 6:
                ada_unit(0, b)
        ada_finish(0)


        def end_phase(li, xsrc, src_name, after_block=None):
            final = (li == nl - 1)
            pend = []

            def slot(b, d):
                return (8 * b + d) % NXO

            def load_x(b, d):
                sl = slot(b, d)
                dma(XOS[sl], xsrc[d * 128:(d + 1) * 128, bslice(b)],
                    [(src_name, d, b)] if src_name else (), [("xo", sl)])

            def load_y(b):
                dma(YRB[b % 2], ysc_d[:, :, bslice(b)].rearrange("c p t -> p c t"),
                    [("ysc", c, b) for c in range(8)], [("yrb", b % 2)])

            def compute_d(b, d):
                yb = b % 2
                sl = slot(b, d)
                xo = XOS[sl]
                if li >= 0:
                    pbk = next_pb()
                    for c in range(8):
                        mm(PS[pbk][:], WOUT[:, c, d * 128:(d + 1) * 128], YRB[yb][:, c, :], c == 0, c == 7,
                           [("wout",), ("yrb", yb)], [("ps", pbk)])
                    stt(xo, PS[pbk][:], gate_col(li, d), xo, ALU.mult, ALU.add,
                        [("ps", pbk), ("modT",), ("xo", sl)], [("xo", sl)])
                    if not final:
                        dma(xs_d[d * 128:(d + 1) * 128, bslice(b)], xo, [("xo", sl)], [("xs", d, b)])
                if final and not last:
                    dma(out_d[d * 128:(d + 1) * 128, bslice(b)], xo, [("xo", sl)], [("out", d, b)])
                    return
                si = d % 2
                act(SQ[si][:], xo, AF.Square, [("xo", sl)], [("sq", si)])
                pend.append((d, si))
                if len(pend) > 1:
                    dd, ss_ = pend.pop(0)
                    mm(PS[MB][:], ONESB[:], SQ[ss_][:], dd == 0, dd == 7, [("onesb",), ("sq", ss_)], [("ps", MB)])

            def finalize(b):
                for d in range(8):
                    sl = slot(b, d)
                    xo = XOS[sl]
                    if not final:
                        tt("dve", xo, xo, T[6][:], ALU.mult, [("xo", sl), ("T", 6)], [("xo", sl)])
                        act(hT[:, d, bslice(b)], xo, AF.Identity, [("xo", sl), ("gp1",), ("modT",)],
                            [("hT", d, b)], bias=shift_col(li + 1, d), scale=gp1_col(li + 1, d))
                    else:
                        stt(xo, xo, col("fg", d), T[6][:], ALU.mult, ALU.mult,
                            [("xo", sl), ("cols",), ("T", 6)], [("xo", sl)])
                        dma(out_d[d * 128:(d + 1) * 128, bslice(b)], xo, [("xo", sl)], [("out", d, b)])
                    if d < 4 and b + 1 < NB:
                        load_x(b + 1, d + 4)

            if li >= 0:
                load_y(0)
            for d in range(8):
                load_x(0, d)
            for d in range(8):
                compute_d(0, d)
            for b in range(NB):
                if b + 1 < NB:
                    if li >= 0:
                        load_y(b + 1)
                    for d in range(4):
                        load_x(b + 1, d)
                if b in (0, 1):
                    flush_tail()
                if final and not last:
                    if b + 1 < NB:
                        for d in range(4, 8):
                            load_x(b + 1, d)
                        for d in range(8):
                            compute_d(b + 1, d)
                    continue
                while pend:
                    dd, ss_ = pend.pop(0)
                    mm(PS[MB][:], ONESB[:], SQ[ss_][:], dd == 0, dd == 7, [("onesb",), ("sq", ss_)], [("ps", MB)])
                act(T[6][:], PS[MB][:], AF.Ln, [("ps", MB)], [("T", 6)], bias=EPS, scale=1.0 / D)
                act(T[6][:], T[6][:], AF.Exp, [("T", 6)], [("T", 6)], scale=-0.5)
                if b + 1 < NB:
                    for d in range(4):
                        compute_d(b + 1, d)
                finalize(b)
                if b + 1 < NB:
                    for d in range(4, 8):
                        compute_d(b + 1, d)
                if after_block is not None:
                    after_block(b)

        def attention_pair(kd, mask, chunk, wg, prefetch=None, extra=None):
            pus = [(I, kb) for I in range(NB) for kb in range(4 * (I + 1))]
            n = len(pus)
            deferred = []

            def obank(I, hd):
                return 4 + (2 * I + hd) % 3

            def gate_proj(I, pu):
                gb = I % 2
                for k in range(8):
                    mm(PS[GB][:], wg[:, k, :], hT[:, k, bslice(I)], k == 0, k == 7,
                       [("WP",), ("hT", k, I)], [("ps", GB)])
                cp("dve", GG[gb][:], PS[GB][:], [("ps", GB)], [("G", gb)])

                def later():
                    act(EE[gb][:], PS[GB][:], AF.Exp, [("ps", GB)], [("E", gb)] + ([("ps", GB)] if _ordrr else []),
                        scale=-1.0)
                deferred.append((pu + 1, later))

            def emit_S(pu):
                I, kb = pus[pu]
                if kb == 0:
                    gate_proj(I, pu)
                    if I == NB - 1 and prefetch is not None and not _nopf:
                        prefetch()
                jd = kb - 4 * I
                qlo = 128 * max(0, jd)
                j = pu % 2
                for hd in (0, 1):
                    bk = 2 * j + hd
                    mm(PS[bk][:, qlo:512], KT[hd][0:kd[hd], kb * 128:(kb + 1) * 128],
                       QT[hd][0:kd[hd], I * BLK + qlo:(I + 1) * BLK], True, jd < 0,
                       [("KT", hd, kb // 4), ("QT", hd, I)], [("ps", bk)])
                    if jd >= 0:
                        mm(PS[bk][:, qlo:qlo + 128], ident, mask, False, True, [("cbf",)], [("ps", bk)])

            def epilogue(I, pu):
                eb = I % 2
                oa, ob = obank(I, 0), obank(I, 1)
                cp("dve", OT[eb][0][:], PS[oa][:], [("ps", oa)], [("Ot", eb, 0)])
                cp("dve", OT[eb][1][:], PS[ob][:], [("ps", ob)], [("Ot", eb, 1)])
                dma(LM[eb][0:64, :], OT[eb][0][64:128, :], [("Ot", eb, 0)], [("Lm", eb)])
                dma(LM[eb][64:128, :], OT[eb][1][0:64, :], [("Ot", eb, 1)], [("Lm", eb)])

                def later():
                    stt(LM[eb][:], EE[eb][:], 1.0, LM[eb][:], ALU.add, ALU.mult,
                        [("E", eb), ("Lm", eb)], [("Lm", eb)])
                    if I == NB - 1:
                        act(LM[eb][:], LM[eb][:], AF.Ln, [("Lm", eb)], [("Lm", eb)])
                        act(LM[eb][:], LM[eb][:], AF.Exp, [("Lm", eb)], [("Lm", eb)], scale=-1.0)
                        tail_next.append(later2)
                        return
                    A("dve", lambda e: e.reciprocal(out=LM[eb][:], in_=LM[eb][:]), [("Lm", eb)], [("Lm", eb)])
                    later2()

                def later2():
                    em = "dve" if I == NB - 1 else "pool"
                    tt(em, LM[eb][:], LM[eb][:], GG[eb][:], ALU.mult, [("Lm", eb), ("G", eb)], [("Lm", eb)])
                    tt(em, YS[eb][0:64, :], OT[eb][0][0:64, :], LM[eb][0:64, :], ALU.mult,
                       [("Ot", eb, 0), ("Lm", eb)], [("ys", eb)])
                    tt(em, YS[eb][64:128, :], OT[eb][1][64:128, :], LM[eb][64:128, :], ALU.mult,
                       [("Ot", eb, 1), ("Lm", eb)], [("ys", eb)])
                    dma(ysc_d[chunk, :, bslice(I)], YS[eb][:], [("ys", eb)], [("ysc", chunk, I)])
                deferred.append((pu + 4, later))

            def emit_rest(pu):
                I, kb = pus[pu]
                jd = kb - 4 * I
                qlo = 128 * max(0, jd)
                N = 512 - qlo
                j = pu % 2
                pt = pu % 3
                for hd in (0, 1):
                    act(PT[pt][:, hd * 512: hd * 512 + N], PS[2 * j + hd][:, qlo:512], AF.Exp,
                        [("ps", 2 * j + hd)], [("PT", pt, hd)])
                lastkb = (kb == 4 * (I + 1) - 1)
                for hd in (0, 1):
                    ok = obank(I, hd)
                    lhsT = VV[:, kb, 0:128] if hd == 0 else VV[:, kb, 64:192]
                    mm(PS[ok][:, qlo:512], lhsT, PT[pt][:, hd * 512: hd * 512 + N], kb == 0, lastkb,
                       [("PT", pt, hd), ("VV", kb // 4)], [("ps", ok)])
                if lastkb:
                    epilogue(I, pu)

            emit_S(0)
            for pu in range(n):
                if pu + 1 < n:
                    emit_S(pu + 1)
                if extra is not None:
                    extra(pu, deferred)
                emit_rest(pu)
                keep = []
                for due, fn in deferred:
                    if due <= pu:
                        fn()
                    else:
                        keep.append((due, fn))
                deferred[:] = keep
            tail_fns.extend(fn for due, fn in deferred)
            if prefetch is not None and _nopf:
                prefetch()

        tail_fns = []
        tail_next = []

        def flush_tail():
            fns = list(tail_fns)
            del tail_fns[:]
            for fn in fns:
                fn()
            tail_fns.extend(tail_next)
            del tail_next[:]

        def v_evac(pbk, g):
            pv = PS[pbk][:].rearrange("p (j c) -> p j c", c=128)
            cp("dve", VV[:, 4 * g:4 * g + 4, 0:64], pv[:, :, 0:64], [("ps", pbk)], [("VV", g)])
            act(VV[:, 4 * g:4 * g + 4, 128:192], pv[:, :, 64:128], AF.Copy, [("ps", pbk)],
                [("VV", g)] + ([("ps", pbk)] if _ordrr else []))

        try:
          ada_units = [(l, cb) for l in range(1, nl) for cb in range(6)]
          per_blk = (len(ada_units) + NB - 1) // NB

          def after_block(b):
              for _ in range(per_blk):
                  if ada_units:
                      l_, cb_ = ada_units.pop(0)
                      ada_unit(l_, cb_)
                      if cb_ == 5:
                          ada_finish(l_)

          end_phase(-1, xT_in, None, after_block)
          while ada_units:
              after_block(0)
          sch.alias(["wst"], ["qn", "kvn", "kr"])
          stop_at("pro")
          for l in range(nl):
              sch.alias(["xo", "yrb"], ["QT", "KT", "VV"])
              mset("pool", VV[:, :, 64:128], 1.0, [("VV", g) for g in range(8)])
              WC_ql = WC[:, 0:2048].rearrange("p (k c) -> p k c", c=256)
              WC_kvl = WC[:, 2048:3072].rearrange("p (k c) -> p k c", c=128)
              WC_kr = WC[:, 3072:3328].rearrange("p (k c) -> p k c", c=32)
              WC_krr = WC[:, 3328:3584].rearrange("p (k c) -> p k c", c=32)
              WC_ff = WC[:, 3584:4160].rearrange("p (k c) -> p k c", c=72)

              def load_wc(ll, WC_krr=WC_krr):
                  dma(WC[:], wcom_d[ll], (), [("WC",)], q="pool", max_dma_last_dim=4096)
                  ts("pool", WC_krr[:, :, 0:16], WC_krr[:, :, 0:16], -1.0, None, ALU.mult, None,
                     [("WC",)], [("WC",)])

              if l == 0:
                  load_wc(0)
              for b in range(NB):
                  dma(RC[0:96, :], rC_d[:, bslice(b)], [("ropeC", b)], [("rc",)])
                  dma(RS[0:96, :], rS_d[:, bslice(b)], [("ropeS", b)], [("rsn",)])
                  pq = []
                  for c in range(2):
                      pbk = next_pb()
                      pq.append(pbk)
                      for k in range(8):
                          mm(PS[pbk][:], WC_ql[:, k, c * 128:(c + 1) * 128], hT[:, k, bslice(b)], k == 0, k == 7,
                             [("WC",), ("hT", k, b)], [("ps", pbk)])
                      cp("dve", T[c][:], PS[pbk][:], [("ps", pbk)], [("T", c)])
                      act(SQ[c][:], T[c][:], AF.Square, [("T", c)], [("sq", c)])
                  pkv = next_pb()
                  for k in range(8):
                      mm(PS[pkv][:], WC_kvl[:, k, :], hT[:, k, bslice(b)], k == 0, k == 7,
                         [("WC",), ("hT", k, b)], [("ps", pkv)])
                  cp("dve", T[2][:], PS[pkv][:], [("ps", pkv)], [("T", 2)])
                  p1 = next_pb()
                  for k in range(8):
                      mm(PS[p1][0:32, :], WC_kr[:, k, :], hT[:, k, bslice(b)], k == 0, k == 7,
                         [("WC",), ("hT", k, b)], [("ps", p1)])
                  tt("dve", T[3][0:32, :], PS[p1][0:32, :], RC[0:32, :], ALU.mult, [("ps", p1), ("rc",)], [("T", 3)])
                  p2 = next_pb()
                  for k in range(8):
                      mm(PS[p2][0:32, :], WC_krr[:, k, :], hT[:, k, bslice(b)], k == 0, k == 7,
                         [("WC",), ("hT", k, b)], [("ps", p2)])
                  tt("dve", T[4][0:32, :], PS[p2][0:32, :], RS[0:32, :], ALU.mult, [("ps", p2), ("rsn",)], [("T", 4)])
                  tt("pool", KR[0:32, bslice(b)], T[3][0:32, :], T[4][0:32, :], ALU.add,
                     [("T", 3), ("T", 4)], [("kr", b)])
                  for c in range(2):
                      mm(PS[MB][:], ONESB[:], SQ[c][:], c == 0, c == 1, [("onesb",), ("sq", c)], [("ps", MB)])
                  act(T[6][:], PS[MB][:], AF.Ln, [("ps", MB)], [("T", 6)], bias=EPS, scale=1.0 / 256)
                  act(T[6][:], T[6][:], AF.Exp, [("T", 6)], [("T", 6)], scale=-0.5)
                  for c in range(2):
                      stt(QN[:, c, bslice(b)], T[c][:], col("qg", l * 2 + c), T[6][:], ALU.mult, ALU.mult,
                          [("T", c), ("cols",), ("T", 6)], [("qn", b)])
                  act(SQ[0][:], T[2][:], AF.Square, [("T", 2)], [("sq", 0)])
                  mm(PS[MB2][:], ONESB[:], SQ[0][:], True, True, [("onesb",), ("sq", 0)], [("ps", MB2)])
                  act(T[5][:], PS[MB2][:], AF.Ln, [("ps", MB2)], [("T", 5)], bias=EPS, scale=1.0 / 128)
                  act(T[5][:], T[5][:], AF.Exp, [("T", 5)], [("T", 5)], scale=-0.5)
                  stt(KVN[:, bslice(b)], T[2][:], col("kvg", l), T[5][:], ALU.mult, ALU.mult,
                      [("T", 2), ("cols",), ("T", 5)], [("kvn", b)])

              stop_at("mlacom")
              def load_mla(p, l=l):
                  dma(WP[:, 0:W_MLA], wmla_d[l, p], (), [("WP",)], q="pool", max_dma_last_dim=4096)

              def load_fox(p, l=l):
                  dma(WP[:, 0:W_FOX], wfox_d[l, p], (), [("WP",)], q="pool", max_dma_last_dim=4096)

              load_mla(0)
              for p in range(4):
                  uq = [WP[:, h * 384: h * 384 + 192].rearrange("p (k c) -> p k c", c=96) for h in range(2)]
                  uqr = [WP[:, h * 384 + 192: h * 384 + 384].rearrange("p (k c) -> p k c", c=96) for h in range(2)]
                  uk = [WP[:, 768 + h * 96: 768 + (h + 1) * 96] for h in range(2)]
                  uv = WP[:, 960:1088]
                  wg = WP[:, 1088:2112].rearrange("p (k c) -> p k c", c=128)
                  for h in range(2):
                      act(uqr[h][:, :, 64:80], uqr[h][:, :, 64:80], AF.Copy, [("WP",)], [("WP",)], scale=-1.0)
                  for b in range(NB):
                      dma(RC[0:96, :], rC_d[:, bslice(b)], [("ropeC", b)], [("rc",)])
                      dma(RS[0:96, :], rS_d[:, bslice(b)], [("ropeS", b)], [("rsn",)])
                      for h in range(2):
                          p1 = next_pb()
                          for k in range(2):
                              mm(PS[p1][0:96, :], uq[h][:, k, :], QN[:, k, bslice(b)], k == 0, k == 1,
                                 [("WP",), ("qn", b)], [("ps", p1)])
                          tA, kA = (T[3], ("T", 3)) if h == 0 else (OT[0][0], ("Ot", 0, 0))
                          tB, kB = (T[4], ("T", 4)) if h == 0 else (OT[0][1], ("Ot", 0, 1))
                          tt("dve", tA[64:96, :], PS[p1][64:96, :], RC[64:96, :], ALU.mult,
                             [("ps", p1), ("rc",)], [kA])
                          act(QT[h][0:64, bslice(b)], PS[p1][0:64, :], AF.Copy, [("ps", p1)],
                              [("QT", h, b), ("ps", p1)], scale=MLA_SCALE)
                          p2 = next_pb()
                          for k in range(2):
                              mm(PS[p2][0:96, :], uqr[h][:, k, :], QN[:, k, bslice(b)], k == 0, k == 1,
                                 [("WP",), ("qn", b)], [("ps", p2)])
                          tt("dve", tB[64:96, :], PS[p2][64:96, :], RS[64:96, :], ALU.mult,
                             [("ps", p2), ("rsn",)], [kB])
                          tt("pool", QT[h][64:96, bslice(b)], tA[64:96, :], tB[64:96, :], ALU.add,
                             [kA, kB], [("QT", h, b)])
                          pk = next_pb()
                          mm(PS[pk][0:96, :], uk[h], KVN[:, bslice(b)], True, False,
                             [("WP",), ("kvn", b)], [("ps", pk)])
                          mm(PS[pk][0:96, :], selr, KR[0:32, bslice(b)], False, True,
                             [("cbf",), ("kr", b)], [("ps", pk)])
                          act(KT[h][0:96, bslice(b)], PS[pk][0:96, :], AF.Copy, [("ps", pk)], [("KT", h, b)])
                      pv = next_pb()
                      for j in range(4):
                          t = 4 * b + j
                          mm(PS[pv][:, j * 128:(j + 1) * 128], KVN[:, t * 128:(t + 1) * 128], uv, True, True,
                             [("WP",), ("kvn", b)], [("ps", pv)], skip=True)
                      v_evac(pv, b)
                      if b in (1, 2):
                          flush_tail()
                  stop_at("mlaproj")
                  extra = None
                  if p == 3:
                      sch.alias(["kvn"], ["C"])
                      sch.alias(["qn"], ["wout"])
                      dma(WOUT, wout_d[l].rearrange("p (k c) -> p k c", c=1024), (), [("wout",)], q="pool",
                          max_dma_last_dim=4096)
                      mset("pool", CC[0:32, :], 1.0, [("C", b) for b in range(NB)])
                      mset("dve", T[5][:], 1.0, [("T", 5)])

                      def fox_common(b, deferred, pu, l=l, WC_ff=WC_ff):
                          for k in range(8):
                              mm(PS[GB][0:72, :], WC_ff[:, k, :], hT[:, k, bslice(b)], k == 0, k == 7,
                                 [("WC",), ("hT", k, b)], [("ps", GB)])

                          def later():
                              act(T[0][0:72, :], PS[GB][0:72, :], AF.Exp, [("ps", GB), ("nbf",)],
                                  [("T", 0), ("ps", GB)], bias=NBF[0:72, l:l + 1], scale=-1.0)
                              act(T[0][0:72, :], T[0][0:72, :], AF.Ln, [("T", 0)], [("T", 0)], bias=1.0, scale=1.0)
                              cur, prv = T[1 + b % 2], T[1 + (b + 1) % 2]
                              kc, kp = ("T", 1 + b % 2), ("T", 1 + (b + 1) % 2)
                              init = 0.0 if b == 0 else prv[0:72, 511:512]
                              A("dve", lambda e, cur=cur, init=init: e.tensor_tensor_scan(
                                  out=cur[0:72, :], data0=T[5][0:72, :], data1=T[0][0:72, :], initial=init,
                                  op0=ALU.mult, op1=ALU.add), [("T", 5), ("T", 0), kp], [kc])
                              cp("pool", HB[0][0:72, :], cur[0:72, :], [kc], [("sq", 0)])
                              cp("pool", CC[0:8, bslice(b)], HB[0][0:8, :], [("sq", 0)], [("C", b)])
                              tt("pool", T[3][0:72, :], cur[0:72, :], HB[0][0:72, :], ALU.subtract,
                                 [kc, ("sq", 0)], [("T", 3)])
                              cp("pool", HB[1][0:72, :], T[3][0:72, :], [("T", 3)], [("sq", 1)])
                              cp("pool", CC[32:40, bslice(b)], HB[1][32:40, :], [("sq", 1)], [("C", b)])
                              tt("pool", T[4][0:72, :], T[3][0:72, :], HB[1][0:72, :], ALU.subtract,
                                 [("T", 3), ("sq", 1)], [("T", 4)])
                              cp("pool", CC[64:72, bslice(b)], T[4][64:72, :], [("T", 4)], [("C", b)])
                          deferred.append((pu + 1, later))

                      fc_sched = {6: 0, 14: 1, 26: 2, 42: 3, 62: 4, 86: 5, 114: 6, 122: 7}

                      def extra(pu, deferred):
                          if pu in fc_sched:
                              fox_common(fc_sched[pu], deferred, pu)
                  attention_pair((96, 96), maskM, 4 + p, wg,
                                 prefetch=(lambda p=p: load_mla(p + 1)) if p < 3 else (lambda: load_fox(0)),
                                 extra=extra)
                  if p == 3 and l + 1 < nl:
                      load_wc(l + 1)

              stop_at("mla")
              stop_at("foxcom")
              for p in range(4):
                  wq = WP[:, 0:1024].rearrange("p (k c) -> p k c", c=128)
                  wk = WP[:, 1024:2048].rearrange("p (k c) -> p k c", c=128)
                  wv = WP[:, 2048:3072].rearrange("p (k c) -> p k c", c=128)
                  wg = WP[:, 3072:4096].rearrange("p (k c) -> p k c", c=128)
                  for b in range(NB):
                      if p == 0:
                          mset("pool", QT[1][0:64, bslice(b)], 0.0, [("QT", 1, b)])
                          mset("pool", KT[1][0:64, bslice(b)], 0.0, [("KT", 1, b)])
                      for (wt, sel, dst, nm, sc) in ((wq, selq[p], QT, "QT", 0.125), (wk, selk[p], KT, "KT", None)):
                          pm = next_pb()
                          for k in range(8):
                              mm(PS[pm][:], wt[:, k, :], hT[:, k, bslice(b)], k == 0, k == 7,
                                 [("WP",), ("hT", k, b)], [("ps", pm)])
                          pa = next_pb()
                          mm(PS[pa][0:70, :], sel, CC[0:72, bslice(b)], True, True,
                             [("cbf",), ("C", b)], [("ps", pa)])
                          if sc is not None:
                              ts("dve", dst[0][0:64, bslice(b)], PS[pm][0:64, :], sc, None, ALU.mult, None,
                                 [("ps", pm)], [(nm, 0, b)])
                              ts("dve", dst[1][64:128, bslice(b)], PS[pm][64:128, :], sc, None, ALU.mult, None,
                                 [("ps", pm)], [(nm, 1, b)])
                              ts("dve", dst[0][64:70, bslice(b)], PS[pa][64:70, :], sc, None, ALU.mult, None,
                                 [("ps", pa)], [(nm, 0, b)])
                              ts("dve", dst[1][0:6, bslice(b)], PS[pa][0:6, :], sc, None, ALU.mult, None,
                                 [("ps", pa)], [(nm, 1, b)])
                          else:
                              act(dst[0][0:64, bslice(b)], PS[pm][0:64, :], AF.Copy, [("ps", pm)], [(nm, 0, b)])
                              act(dst[1][64:128, bslice(b)], PS[pm][64:128, :], AF.Copy, [("ps", pm)], [(nm, 1, b)])
                              act(dst[0][64:70, bslice(b)], PS[pa][64:70, :], AF.Copy, [("ps", pa)], [(nm, 0, b)])
                              act(dst[1][0:6, bslice(b)], PS[pa][0:6, :], AF.Copy, [("ps", pa)], [(nm, 1, b)])
                      pv = next_pb()
                      for j in range(4):
                          t = 4 * b + j
                          for k in range(8):
                              mm(PS[pv][:, j * 128:(j + 1) * 128], hT[:, k, t * 128:(t + 1) * 128], wv[:, k, :],
                                 k == 0, k == 7, [("WP",), ("hT", k, b)], [("ps", pv)], skip=True)
                      v_evac(pv, b)
                      if b in (1, 2):
                          flush_tail()
                  stop_at(f"foxproj{p}")
                  attention_pair((70, 128), maskF, p, wg, prefetch=(lambda p=p: load_fox(p + 1)) if p < 3 else None)
                  stop_at(f"foxatt{p}")

              stop_at("fox")
              sch.alias(["QT", "KT", "VV"], ["xo", "yrb"])
              end_phase(l, xT_in if l == 0 else xs_d, None if l == 0 else "xs")
              sch.alias(["C"], ["kvn"])
              sch.alias(["wout"], ["qn"])

        except _Stop:
            pass

        A("sp", None, [("out", d, b) for d in range(8) for b in range(NB)], ())

        sem_eng = {e: es.enter_context(nc.semaphore(f"s_{e}")) for e in ("pe", "act", "dve", "pool")}
        dma_sems = {
            "sp": [es.enter_context(nc.semaphore(f"d_sp{i}")) for i in range(16)],
            "pool": [es.enter_context(nc.semaphore(f"d_pl{i}")) for i in range(6)],
        }
        sch.finalize(sem_eng, dma_sems)
        block = es.enter_context(nc.Block())

        @block.tensor
        def _(e):
            sch.emit("pe", e)

        @block.scalar
        def _(e):
            sch.emit("act", e)

        @block.vector
        def _(e):
            sch.emit("dve", e)

        @block.gpsimd
        def _(e):
            sch.emit("pool", e)

        @block.sync
        def _(e):
            sch.emit("sp", e)

    return nc


def _consts_bf():
    cb = np.zeros((128, CB_N), np.float32)
    cb[:, CB_ID:CB_ID + 128] = np.eye(128, dtype=np.float32)
    k = np.arange(128)[:, None]
    q = np.arange(128)[None, :]
    cb[:, CB_MF:CB_MF + 128] = np.where(k <= q, 0.0, -30000.0)
    cb[:, CB_MM:CB_MM + 128] = np.where((k // 64) <= (q // 64), 0.0, -30000.0)
    for p in range(4):
        ha, hb = 2 * p, 2 * p + 1
        sq = np.zeros((128, 70), np.float32)
        sk = np.zeros((128, 70), np.float32)
        for j, g in enumerate((0, 32, 64)):
            sq[g + ha, 64 + j] = -8.0
            sq[g + hb, 0 + j] = -8.0
            sk[g + ha, 67 + j] = 1.0
            sk[g + hb, 3 + j] = 1.0
        sq[8, 67:70] = 8.0
        sq[8, 3:6] = 8.0
        sk[8, 64:67] = 1.0
        sk[8, 0:3] = 1.0
        cb[:, CB_SQ + p * 70: CB_SQ + (p + 1) * 70] = sq
        cb[:, CB_SK + p * 70: CB_SK + (p + 1) * 70] = sk
    sr = np.zeros((128, 96), np.float32)
    for r in range(32):
        sr[r, 64 + r] = 1.0
    cb[:, CB_SR:CB_SR + 96] = sr
    return cb


def _kchunks(w):
    kk = w.shape[0] // 128
    return w.reshape(kk, 128, w.shape[1]).transpose(1, 0, 2)


def _pack_weights(layers, w_in, w_uq, w_ukv, w_out, w_ada, b_ada):
    nl = len(layers)
    wfox = np.zeros((nl, 4, 128, W_FOX), np.float32)
    wmla = np.zeros((nl, 4, 128, W_MLA), np.float32)
    wcom = np.zeros((nl, 128, W_COM), np.float32)
    wout = np.zeros((nl, 128, 8192), np.float32)
    wada = np.zeros((nl, 6, 128, 4096), np.float32)
    bada = np.zeros((nl, 6, 1, 512), np.float32)
    for i, l in enumerate(layers):
        wi = _kchunks(w_in[l])
        uqc = _kchunks(w_uq[l])
        ukv = w_ukv[l]
        for p in range(4):
            buf = np.zeros((128, W_FOX), np.float32)
            buf[:, 0:1024] = wi[:, :, p * 128:(p + 1) * 128].reshape(128, 1024)
            buf[:, 1024:2048] = wi[:, :, 512 + p * 128: 512 + (p + 1) * 128].reshape(128, 1024)
            buf[:, 2048:3072] = wi[:, :, 1024 + p * 128: 1024 + (p + 1) * 128].reshape(128, 1024)
            buf[:, 3072:4096] = wi[:, :, 1544 + p * 128: 1544 + (p + 1) * 128].reshape(128, 1024)
            wfox[i, p] = buf
            buf = np.zeros((128, W_MLA), np.float32)
            for h in range(2):
                hd = 2 * p + h
                buf[:, h * 384: h * 384 + 192] = uqc[:, :, hd * 96:(hd + 1) * 96].reshape(128, 192)
                t = np.zeros((128, 2, 96), np.float32)
                t[:, :, 64:80] = uqc[:, :, hd * 96 + 80: hd * 96 + 96]
                t[:, :, 80:96] = uqc[:, :, hd * 96 + 64: hd * 96 + 80]
                buf[:, h * 384 + 192: h * 384 + 384] = t.reshape(128, 192)
                buf[:, 768 + h * 96: 768 + h * 96 + 64] = ukv[:, hd * 128: hd * 128 + 64]
                buf[:, 960 + h * 64: 960 + (h + 1) * 64] = ukv[:, hd * 128 + 64: hd * 128 + 128]
            buf[:, 1088:2112] = wi[:, :, 2472 + p * 128: 2472 + (p + 1) * 128].reshape(128, 1024)
            wmla[i, p] = buf
        buf = np.zeros((128, W_COM), np.float32)
        buf[:, 0:2048] = wi[:, :, 2056:2312].reshape(128, 2048)
        buf[:, 2048:3072] = wi[:, :, 2312:2440].reshape(128, 1024)
        buf[:, 3072:3328] = wi[:, :, 2440:2472].reshape(128, 256)
        t = np.zeros((128, 8, 32), np.float32)
        t[:, :, 0:16] = wi[:, :, 2456:2472]
        t[:, :, 16:32] = wi[:, :, 2440:2456]
        buf[:, 3328:3584] = t.reshape(128, 256)
        t = np.zeros((128, 8, 72), np.float32)
        for g in (0, 32, 64):
            t[:, :, g:g + 8] = wi[:, :, 1536:1544]
        buf[:, 3584:4160] = t.reshape(128, 576)
        wcom[i] = buf
        wout[i] = _kchunks(w_out[l]).reshape(128, 8192)
        wa = _kchunks(w_ada[l])
        for cb in range(6):
            wada[i, cb] = wa[:, :, cb * 512:(cb + 1) * 512].reshape(128, 4096)
            bada[i, cb, 0] = b_ada[l][cb * 512:(cb + 1) * 512]
    return wfox, wmla, wcom, wout, wada, bada


def _pack_cols(layers, b, c, norm_g, final_g, q_norm_g, kv_norm_g, b_f):
    nl = len(layers)
    CL = _col_layout(nl)
    cols = np.zeros((128, CL["_n"]), np.float32)
    for i, l in enumerate(layers):
        cols[:, CL["ng"] + i * 8: CL["ng"] + (i + 1) * 8] = norm_g[l].reshape(8, 128).T
        cols[:, CL["qg"] + i * 2: CL["qg"] + (i + 1) * 2] = q_norm_g[l].reshape(2, 128).T
        cols[:, CL["kvg"] + i] = kv_norm_g[l]
        for g in (0, 32, 64):
            cols[g:g + 8, CL["bf"] + i] = b_f[l]
    cols[:, CL["fg"]:CL["fg"] + 8] = final_g.reshape(8, 128).T
    inv_freq = (1.0 / (10000.0 ** (np.arange(0, 32, 2, dtype=np.float32) / np.float32(32)))).astype(np.float32)
    for base in (0, 64):
        cols[base:base + 16, CL["invf"]] = inv_freq
        cols[base + 16:base + 32, CL["invf"]] = inv_freq
    cols[0:64, CL["scl"]] = 1.0
    cols[64:128, CL["scl"]] = MLA_SCALE
    cols[:, CL["cc"]:CL["cc"] + 8] = c[b].reshape(8, 128).T
    return cols


_PROG_CACHE = {}


def _program(nl, last):
    key = (nl, last)
    if key not in _PROG_CACHE:
        _PROG_CACHE[key] = build_program(nl, last)
    return _PROG_CACHE[key]


def _launch(layers, last, xT_list, inputs, cbf):
    nl = len(layers)
    wfox, wmla, wcom, wout, wada, bada = _pack_weights(
        layers, inputs["w_in"], inputs["w_uq"], inputs["w_ukv"], inputs["w_out"], inputs["w_ada"],
        inputs["b_ada"])
    in_maps = []
    for b in range(8):
        in_maps.append({
            "xT": xT_list[b],
            "pos": np.ascontiguousarray(inputs["positions"][b].reshape(1, S)).astype(np.int32),
            "cols": _pack_cols(layers, b, inputs["c"], inputs["norm_g"], inputs["final_g"],
                               inputs["q_norm_g"], inputs["kv_norm_g"], inputs["b_f"]),
            "cbf": cbf, "wfox": wfox, "wmla": wmla, "wcom": wcom, "wout": wout, "wada": wada, "bada": bada,
        })
    nc = _program(nl, last)
    res = run_bass_kernel_spmd(nc, in_maps, core_ids=list(range(8)))
    return [np.asarray(r["out"]) for r in res.results]


LAYER_GROUPS = [[0, 1, 2, 3]]


def kernel(**inputs):
    inputs = {k: np.asarray(v) for k, v in inputs.items()}
    x = inputs["x"].astype(np.float32, copy=False)
    cbf = _consts_bf()
    xT = [np.ascontiguousarray(x[b].T) for b in range(8)]
    for gi, layers in enumerate(LAYER_GROUPS):
        xT = _launch(layers, gi == len(LAYER_GROUPS) - 1, xT, inputs, cbf)
    out = np.stack([np.ascontiguousarray(xT[b].T) for b in range(8)], axis=0)
    return out.astype(np.float32, copy=False)
```

```python
import numpy as np
from contextlib import ExitStack
import concourse.bass as bass
import concourse.mybir as mybir
from concourse.bass_utils import run_bass_kernel_spmd

F32 = mybir.dt.float32
BF16 = mybir.dt.bfloat16
I32 = mybir.dt.int32
AF = mybir.ActivationFunctionType
ALU = mybir.AluOpType

D = 1024
S = 4096
NB = 8
BLK = 512
EPS = 1e-6
MLA_SCALE = float(96 ** -0.5)
W_FOX = 4096
W_MLA = 2112
W_COM = 4160
TWO_PI = float(2 * np.pi)
C1 = 6.28125
C2 = float(2 * np.pi - 6.28125)

def _col_layout(nl):
    off = {}
    o = 0
    for name, n in (("ng", nl * 8), ("fg", 8), ("qg", nl * 2), ("kvg", nl), ("bf", nl),
                    ("invf", 1), ("scl", 1), ("cc", 8)):
        off[name] = o
        o += n
    off["_n"] = o
    return off

CB_ID, CB_MF, CB_MM, CB_SQ, CB_SK, CB_SR, CB_N = 0, 128, 256, 384, 664, 944, 1040


class _Op:
    __slots__ = ("eng", "fn", "deps", "idx", "needed", "dma", "sem", "val", "prev")


class Sched:
    ENGS = ("pe", "act", "dve", "pool", "sp")

    def __init__(self):
        self.streams = {e: [] for e in self.ENGS}
        self.keys = {}
        self.seeds = {}

    def _state(self, key):
        st = self.keys.get(key)
        if st is None:
            st = [list(self.seeds.get(key[0], [])), []]
            self.keys[key] = st
        return st

    @staticmethod
    def _reduce(deps, op):
        best = {}
        dmas = {}
        for d in deps:
            if d is op:
                continue
            if d.dma:
                dmas[id(d)] = d
            else:
                b = best.get(d.eng)
                if b is None or d.idx > b.idx:
                    best[d.eng] = d
        return list(best.values()) + list(dmas.values())

    def add(self, eng, fn, reads=(), writes=(), dma=False):
        op = _Op()
        op.eng, op.fn, op.dma, op.needed = eng, fn, dma, False
        op.sem = op.val = op.prev = None
        op.idx = len(self.streams[eng])
        deps = []
        for k in reads:
            deps.extend(self._state(k)[0])
        for k in writes:
            st = self._state(k)
            deps.extend(st[0])
            deps.extend(st[1])
        op.deps = self._reduce(deps, op)
        for k in reads:
            st = self._state(k)
            if op.dma:
                st[1].append(op)
            else:
                st[1] = [r for r in st[1] if r.dma or r.eng != eng] + [op]
        for k in writes:
            st = self._state(k)
            st[0] = [op]
            st[1] = []
        self.streams[eng].append(op)
        return op

    def alias(self, from_names, to_names):
        deps = []
        for k in list(self.keys):
            if k[0] in from_names:
                st = self.keys.pop(k)
                deps.extend(st[0])
                deps.extend(st[1])
        for n in from_names:
            deps.extend(self.seeds.pop(n, []))
        deps = self._reduce(deps, None)
        for n in to_names:
            self.seeds[n] = deps

    @staticmethod
    def _needs_wait(op, d):
        if d.dma:
            return True
        if d.eng == op.eng:
            if op.eng == "pe":
                return False
            if op.idx - d.idx > 3:
                return False
        return True

    def finalize(self, sem_eng, dma_sems):
        for ops in self.streams.values():
            for op in ops:
                if op.dma:
                    op.needed = True
                for d in op.deps:
                    if self._needs_wait(op, d):
                        d.needed = True
        for e, ops in self.streams.items():
            cnt = 0
            k = 0
            pool = dma_sems.get(e, [])
            hist = []
            for op in ops:
                if not op.needed:
                    continue
                if op.dma:
                    n = len(pool)
                    op.sem = pool[k % n]
                    op.val = 16 * (k // n + 1)
                    op.prev = hist[k - n] if k >= n else None
                    hist.append(op)
                    k += 1
                else:
                    cnt += 1
                    op.sem = sem_eng[e]
                    op.val = cnt

    def emit(self, eng, E):
        seen = {}
        seen_dma = set()
        nwait = 0
        for op in self.streams[eng]:
            for d in op.deps:
                if not self._needs_wait(op, d):
                    continue
                if d.dma:
                    if id(d) in seen_dma:
                        continue
                    E.wait_ge(d.sem, d.val)
                    seen_dma.add(id(d))
                    nwait += 1
                else:
                    if seen.get(d.eng, 0) >= d.val:
                        continue
                    E.wait_ge(d.sem, d.val)
                    seen[d.eng] = d.val
                    nwait += 1
            if op.dma and op.needed and op.prev is not None and id(op.prev) not in seen_dma:
                E.wait_ge(op.prev.sem, op.prev.val)
                seen_dma.add(id(op.prev))
                nwait += 1
            if op.fn is None:
                continue
            inst = op.fn(E)
            if op.needed:
                inst.then_inc(op.sem, 16 if op.dma else 1)
        return nwait


def build_program(nl, last):
    nc = bass.Bass("TRN2", target_bir_lowering=False)
    CL = _col_layout(nl)

    def din(name, shape, dt=F32):
        return nc.dram_tensor(name, shape, dt, kind="ExternalInput").ap()

    xT_in = din("xT", [D, S])
    pos_d = din("pos", [1, S], I32)
    cols_d = din("cols", [128, CL["_n"]])
    cbf_d = din("cbf", [128, CB_N])
    wfox_d = din("wfox", [nl, 4, 128, W_FOX])
    wmla_d = din("wmla", [nl, 4, 128, W_MLA])
    wcom_d = din("wcom", [nl, 128, W_COM])
    wout_d = din("wout", [nl, 128, 8192])
    wada_d = din("wada", [nl, 6, 128, 4096])
    bada_d = din("bada", [nl, 6, 1, 512])
    out_d = nc.dram_tensor("out", [D, S], F32, kind="ExternalOutput").ap()
    xs_d = nc.dram_tensor("xs_scr", [D, S], F32, kind="Internal").ap()
    ysc_d = nc.dram_tensor("ysc_scr", [8, 128, S], BF16, kind="Internal").ap()
    rC_d = nc.dram_tensor("ropec_scr", [96, S], F32, kind="Internal").ap()
    rS_d = nc.dram_tensor("ropes_scr", [96, S], F32, kind="Internal").ap()

    sch = Sched()
    A = sch.add
    import os as _os
    _kstop = _os.environ.get("KSTOP", "")
    _nopf = _os.environ.get("K_NOPF", "") == "1"
    _ordrr = True

    class _Stop(Exception):
        pass

    def stop_at(tag):
        if _kstop == tag:
            raise _Stop()

    with ExitStack() as es:
        def sb(name, shape, dt):
            return es.enter_context(nc.sbuf_tensor(name, shape, dt))

        hT_t = sb("hT", [128, 8 * S], BF16)
        hT = hT_t[:].rearrange("p (k t) -> p k t", t=S)
        R1 = sb("R1", [128, 22528], BF16)
        R2 = sb("R2", [128, 16384], BF16)
        WP = sb("WP", [128, W_FOX], BF16)
        WC = sb("WC", [128, W_COM], BF16)
        CBF = sb("CBF", [128, CB_N], BF16)
        COLS = sb("COLS", [128, CL["_n"]], F32)
        MODT = sb("MODT", [128, nl * 24], F32)
        GP1 = sb("GP1", [128, nl * 8], F32)
        NBF = sb("NBF", [128, nl], F32)
        CSIL = sb("CSIL", [128, 8], F32)
        ONESB = sb("ONESB", [128, 128], BF16)
        ONE1 = sb("ONE1", [1, 1], F32)
        NT = 7
        T = [sb(f"T{i}", [128, 512], F32) for i in range(NT)]
        SQ = [sb(f"SQ{i}", [128, 512], BF16) for i in range(2)]
        HB = SQ
        PT = [sb(f"PT{i}", [128, 1024], BF16) for i in range(3)]
        OT = [[sb(f"OT{i}{h}", [128, 512], F32) for h in range(2)] for i in range(2)]
        LM = [sb(f"LM{i}", [128, 512], F32) for i in range(2)]
        EE = [sb(f"EE{i}", [128, 512], F32) for i in range(2)]
        GG = [sb(f"GG{i}", [128, 512], BF16) for i in range(2)]
        YS = [sb(f"YS{i}", [128, 512], BF16) for i in range(2)]
        RC = sb("RC", [128, 512], F32)
        RS = sb("RS", [128, 512], F32)
        PSA = [es.enter_context(nc.psum_tensor(f"psA{i}", [128, 1024], F32)) for i in range(2)]
        PSS = [es.enter_context(nc.psum_tensor(f"ps{i}", [128, 512], F32)) for i in range(4, 8)]
        PS = [PSA[i // 2][:, (i % 2) * 512:(i % 2 + 1) * 512] for i in range(4)] + [t[:] for t in PSS]

        QT = [R1[:, i * S:(i + 1) * S] for i in range(2)]
        KT = [R1[:, 2 * S + i * S: 2 * S + (i + 1) * S] for i in range(2)]
        VV = R1[:, 4 * S: 4 * S + 32 * 192].rearrange("p (t c) -> p t c", c=192)
        NXO = 14
        NPF = 6
        XOS = [R1[:, i * 1024:(i + 1) * 1024].bitcast(F32) for i in range(NXO)]
        YRB = [R1[:, 14336 + i * 4096: 14336 + (i + 1) * 4096].rearrange("p (c t) -> p c t", t=512)
               for i in range(2)]
        QN = R2[:, 0:8192].rearrange("p (k t) -> p k t", t=S)
        KVN = R2[:, 8192:12288]
        KR = R2[:, 12288:16384]
        WOUT = R2[:, 0:8192].rearrange("p (k c) -> p k c", c=1024)
        CC = R2[:, 8192:12288]
        WST = [R2[:, i * 8192:(i + 1) * 8192].bitcast(F32).rearrange("p (k c) -> p k c", c=512)
               for i in range(2)]
        ident = CBF[:, CB_ID:CB_ID + 128]
        maskF = CBF[:, CB_MF:CB_MF + 128]
        maskM = CBF[:, CB_MM:CB_MM + 128]
        selq = [CBF[0:72, CB_SQ + p * 70: CB_SQ + (p + 1) * 70] for p in range(4)]
        selk = [CBF[0:72, CB_SK + p * 70: CB_SK + (p + 1) * 70] for p in range(4)]
        selr = CBF[0:32, CB_SR:CB_SR + 96]

        def col(name, i=0, rows=128):
            o = CL[name] + i
            return COLS[0:rows, o:o + 1]

        MB, MB2, GB = 6, 7, 7
        PBS = [0, 1, 2, 3, 4, 5]
        state = {"pb": 0, "epi": 0}

        def next_pb():
            state["pb"] = (state["pb"] + 1) % len(PBS)
            return PBS[state["pb"]]

        def mm(out, lhsT, rhs, start, stop, reads, writes, skip=False):
            A("pe", lambda e: e.matmul(out, lhsT=lhsT, rhs=rhs, start=start, stop=stop,
                                       skip_group_check=skip), reads, writes)

        def act(out, in_, func, reads, writes, bias=None, scale=None):
            kw = {}
            if bias is not None:
                kw["bias"] = bias
            if scale is not None:
                kw["scale"] = scale
            A("act", lambda e: e.activation(out=out, in_=in_, func=func, **kw), reads, writes)

        def ts(eng, out, in0, s1, s2, op0, op1, reads, writes):
            if op1 is None:
                A(eng, lambda e: e.tensor_scalar(out=out, in0=in0, scalar1=s1, scalar2=None, op0=op0),
                  reads, writes)
            else:
                A(eng, lambda e: e.tensor_scalar(out=out, in0=in0, scalar1=s1, scalar2=s2, op0=op0, op1=op1),
                  reads, writes)

        def tt(eng, out, in0, in1, op, reads, writes):
            A(eng, lambda e: e.tensor_tensor(out=out, in0=in0, in1=in1, op=op), reads, writes)

        def stt(out, in0, scalar, in1, op0, op1, reads, writes):
            A("dve", lambda e: e.scalar_tensor_tensor(out=out, in0=in0, scalar=scalar, in1=in1, op0=op0, op1=op1),
              reads, writes)

        def cp(eng, out, in_, reads, writes):
            A(eng, lambda e: e.tensor_copy(out=out, in_=in_), reads, writes)

        def mset(eng, ap, val, writes):
            A(eng, lambda e: e.memset(ap, val), (), writes)

        def dma(out, in_, reads, writes, q="sp", **kw):
            A(q, lambda e: e.dma_start(out=out, in_=in_, **kw), reads, writes, dma=True)

        def bslice(b):
            return slice(b * BLK, (b + 1) * BLK)

        dma(CBF[:], cbf_d, (), [("cbf",)], q="pool")
        dma(COLS[:], cols_d, (), [("cols",)])
        mset("dve", ONESB[:], 1.0, [("onesb",)])
        mset("dve", ONE1[:], 1.0, [("one1",)])
        ts("dve", NBF[:], COLS[:, CL["bf"]:CL["bf"] + nl], -1.0, None, ALU.mult, None, [("cols",)], [("nbf",)])
        act(CSIL[:], COLS[:, CL["cc"]:CL["cc"] + 8], AF.Exp, [("cols",)], [("csil",)], scale=-1.0)
        ts("dve", CSIL[:], CSIL[:], 1.0, None, ALU.add, None, [("csil",)], [("csil",)])
        A("dve", lambda e: e.reciprocal(out=CSIL[:], in_=CSIL[:]), [("csil",)], [("csil",)])
        tt("dve", CSIL[:], CSIL[:], COLS[:, CL["cc"]:CL["cc"] + 8], ALU.mult, [("csil",), ("cols",)], [("csil",)])

        ada_cnt = [0]

        def ada_unit(l, cb):
            wi = ada_cnt[0] % 2
            ada_cnt[0] += 1
            dma(WST[wi], wada_d[l, cb].rearrange("p (k c) -> p k c", c=512), (), [("wst", wi)])
            dma(T[0][0:1, :], bada_d[l, cb], (), [("T", 0)])
            for k in range(8):
                mm(PS[MB2][0:1, :], CSIL[:, k:k + 1], WST[wi][:, k, :], k == 0, False,
                   [("csil",), ("wst", wi)], [("ps", MB2)])
            mm(PS[MB2][0:1, :], ONE1[0:1, 0:1], T[0][0:1, :], False, True, [("one1",), ("T", 0)], [("ps", MB2)])
            act(T[1][0:1, :], PS[MB2][0:1, :], AF.Copy, [("ps", MB2)], [("T", 1)])
            pbk = next_pb()
            for j in range(4):
                mm(PS[pbk][:, j:j + 1], T[1][0:1, j * 128:(j + 1) * 128], ONE1[0:1, 0:1], True, True,
                   [("T", 1), ("one1",)], [("ps", pbk)], skip=True)
            act(MODT[:, l * 24 + cb * 4: l * 24 + cb * 4 + 4], PS[pbk][:, 0:4], AF.Copy,
                [("ps", pbk)], [("modT",)])

        def ada_finish(l):
            stt(GP1[:, l * 8:(l + 1) * 8], MODT[:, l * 24 + 8: l * 24 + 16], 1.0,
                COLS[:, CL["ng"] + l * 8: CL["ng"] + (l + 1) * 8], ALU.add, ALU.mult,
                [("modT",), ("cols",)], [("gp1",)])

        def shift_col(l, d):
            return MODT[:, l * 24 + d: l * 24 + d + 1]

        def gate_col(l, d):
            return MODT[:, l * 24 + 16 + d: l * 24 + 16 + d + 1]

        def gp1_col(l, d):
            return GP1[:, l * 8 + d: l * 8 + d + 1]

        for b in range(NB):
            PI_ = T[2][0:96, :].bitcast(I32)
            dma(PI_, pos_d[0:1, bslice(b)].broadcast_to([96, BLK]), (), [("T", 2)])
            cp("dve", T[3][0:96, :], PI_, [("T", 2)], [("T", 3)])
            ts("dve", T[3][0:96, :], T[3][0:96, :], col("invf", 0, 96), None, ALU.mult, None,
               [("T", 3), ("cols",)], [("T", 3)])
            ts("dve", T[4][0:96, :], T[3][0:96, :], float(1.0 / TWO_PI), None, ALU.mult, None,
               [("T", 3)], [("T", 4)])
            KI_ = T[2][0:96, :].bitcast(I32)
            cp("dve", KI_, T[4][0:96, :], [("T", 4)], [("T", 2)])
            cp("dve", T[4][0:96, :], KI_, [("T", 2)], [("T", 4)])
            stt(T[3][0:96, :], T[4][0:96, :], -C1, T[3][0:96, :], ALU.mult, ALU.add,
                [("T", 4), ("T", 3)], [("T", 3)])
            stt(T[3][0:96, :], T[4][0:96, :], -C2, T[3][0:96, :], ALU.mult, ALU.add,
                [("T", 4), ("T", 3)], [("T", 3)])

            def wrap(tr, tm):
                ts("dve", T[tm][0:96, :], T[tr][0:96, :], float(np.pi), -TWO_PI, ALU.is_gt, ALU.mult,
                   [("T", tr)], [("T", tm)])
                tt("dve", T[tr][0:96, :], T[tr][0:96, :], T[tm][0:96, :], ALU.add,
                   [("T", tr), ("T", tm)], [("T", tr)])
                ts("dve", T[tm][0:96, :], T[tr][0:96, :], -float(np.pi), TWO_PI, ALU.is_lt, ALU.mult,
                   [("T", tr)], [("T", tm)])
                tt("dve", T[tr][0:96, :], T[tr][0:96, :], T[tm][0:96, :], ALU.add,
                   [("T", tr), ("T", tm)], [("T", tr)])

            wrap(3, 4)
            ts("dve", T[5][0:96, :], T[3][0:96, :], float(np.pi / 2), None, ALU.add, None,
               [("T", 3)], [("T", 5)])
            wrap(5, 4)
            ts("dve", T[3][0:96, :], T[3][0:96, :], 3.141592, -3.141592, ALU.min, ALU.max, [("T", 3)], [("T", 3)])
            ts("dve", T[5][0:96, :], T[5][0:96, :], 3.141592, -3.141592, ALU.min, ALU.max, [("T", 5)], [("T", 5)])
            act(T[3][0:96, :], T[3][0:96, :], AF.Sin, [("T", 3)], [("T", 3)])
            act(T[5][0:96, :], T[5][0:96, :], AF.Sin, [("T", 5)], [("T", 5)])
            ts("dve", T[3][0:96, :], T[3][0:96, :], col("scl", 0, 96), None, ALU.mult, None,
               [("T", 3), ("cols",)], [("T", 3)])
            ts("dve", T[5][0:96, :], T[5][0:96, :], col("scl", 0, 96), None, ALU.mult, None,
               [("T", 5), ("cols",)], [("T", 5)])
            dma(rS_d[:, bslice(b)], T[3][0:96, :], [("T", 3)], [("ropeS", b)])
            dma(rC_d[:, bslice(b)], T[5][0:96, :], [("T", 5)], [("ropeC", b)])
            if b < 6:
                ada_unit(0, b)
        ada_finish(0)


        def end_phase(li, xsrc, src_name, after_block=None):
            final = (li == nl - 1)
            pend = []

            def slot(b, d):
                return (8 * b + d) % NXO

            def load_x(b, d):
                sl = slot(b, d)
                dma(XOS[sl], xsrc[d * 128:(d + 1) * 128, bslice(b)],
                    [(src_name, d, b)] if src_name else (), [("xo", sl)])

            def load_y(b):
                dma(YRB[b % 2], ysc_d[:, :, bslice(b)].rearrange("c p t -> p c t"),
                    [("ysc", c, b) for c in range(8)], [("yrb", b % 2)])

            if li >= 0:
                load_y(0)
            for d in range(8):
                load_x(0, d)
            for b in range(NB):
                yb = b % 2
                if b + 1 < NB:
                    if li >= 0:
                        load_y(b + 1)
                    for d in range(NPF):
                        load_x(b + 1, d)
                for d in range(8):
                    sl = slot(b, d)
                    xo = XOS[sl]
                    if li >= 0:
                        pbk = next_pb()
                        for c in range(8):
                            mm(PS[pbk][:], WOUT[:, c, d * 128:(d + 1) * 128], YRB[yb][:, c, :], c == 0, c == 7,
                               [("wout",), ("yrb", yb)], [("ps", pbk)])
                        stt(xo, PS[pbk][:], gate_col(li, d), xo, ALU.mult, ALU.add,
                            [("ps", pbk), ("modT",), ("xo", sl)], [("xo", sl)])
                        if not final:
                            dma(xs_d[d * 128:(d + 1) * 128, bslice(b)], xo, [("xo", sl)], [("xs", d, b)])
                    if final and not last:
                        dma(out_d[d * 128:(d + 1) * 128, bslice(b)], xo, [("xo", sl)], [("out", d, b)])
                    if not (final and not last):
                        si = d % 2
                        act(SQ[si][:], xo, AF.Square, [("xo", sl)], [("sq", si)])
                        pend.append((d, si))
                        if len(pend) > 1:
                            dd, ss_ = pend.pop(0)
                            mm(PS[MB][:], ONESB[:], SQ[ss_][:], dd == 0, dd == 7, [("onesb",), ("sq", ss_)],
                               [("ps", MB)])
                if b in (0, 1):
                    flush_tail()
                if final and not last:
                    if b + 1 < NB:
                        for d in range(NPF, 8):
                            load_x(b + 1, d)
                    continue
                while pend:
                    dd, ss_ = pend.pop(0)
                    mm(PS[MB][:], ONESB[:], SQ[ss_][:], dd == 0, dd == 7, [("onesb",), ("sq", ss_)], [("ps", MB)])
                act(T[6][:], PS[MB][:], AF.Ln, [("ps", MB)], [("T", 6)], bias=EPS, scale=1.0 / D)
                act(T[6][:], T[6][:], AF.Exp, [("T", 6)], [("T", 6)], scale=-0.5)
                for d in range(8):
                    sl = slot(b, d)
                    xo = XOS[sl]
                    if not final:
                        tt("dve", xo, xo, T[6][:], ALU.mult, [("xo", sl), ("T", 6)], [("xo", sl)])
                        act(hT[:, d, bslice(b)], xo, AF.Identity, [("xo", sl), ("gp1",), ("modT",)],
                            [("hT", d, b)], bias=shift_col(li + 1, d), scale=gp1_col(li + 1, d))
                    else:
                        stt(xo, xo, col("fg", d), T[6][:], ALU.mult, ALU.mult,
                            [("xo", sl), ("cols",), ("T", 6)], [("xo", sl)])
                        dma(out_d[d * 128:(d + 1) * 128, bslice(b)], xo, [("xo", sl)], [("out", d, b)])
                    if d < 8 - NPF and b + 1 < NB:
                        load_x(b + 1, d + NPF)
                if after_block is not None:
                    after_block(b)

        def attention_pair(kd, mask, chunk, wg, prefetch=None, extra=None):
            pus = [(I, kb) for I in range(NB) for kb in range(4 * (I + 1))]
            n = len(pus)
            deferred = []

            def obank(I, hd):
                return 4 + (2 * I + hd) % 3

            def gate_proj(I, pu):
                gb = I % 2
                for k in range(8):
                    mm(PS[GB][:], wg[:, k, :], hT[:, k, bslice(I)], k == 0, k == 7,
                       [("WP",), ("hT", k, I)], [("ps", GB)])
                cp("dve", GG[gb][:], PS[GB][:], [("ps", GB)], [("G", gb)])

                def later():
                    act(EE[gb][:], PS[GB][:], AF.Exp, [("ps", GB)], [("E", gb)] + ([("ps", GB)] if _ordrr else []),
                        scale=-1.0)
                deferred.append((pu + 1, later))

            def emit_S(pu):
                I, kb = pus[pu]
                if kb == 0:
                    gate_proj(I, pu)
                    if I == NB - 1 and prefetch is not None and not _nopf:
                        prefetch()
                jd = kb - 4 * I
                qlo = 128 * max(0, jd)
                j = pu % 2
                for hd in (0, 1):
                    bk = 2 * j + hd
                    mm(PS[bk][:, qlo:512], KT[hd][0:kd[hd], kb * 128:(kb + 1) * 128],
                       QT[hd][0:kd[hd], I * BLK + qlo:(I + 1) * BLK], True, jd < 0,
                       [("KT", hd, kb // 4), ("QT", hd, I)], [("ps", bk)])
                    if jd >= 0:
                        mm(PS[bk][:, qlo:qlo + 128], ident, mask, False, True, [("cbf",)], [("ps", bk)])

            def epilogue(I, pu):
                eb = I % 2
                oa, ob = obank(I, 0), obank(I, 1)
                cp("dve", OT[eb][0][:], PS[oa][:], [("ps", oa)], [("Ot", eb, 0)])
                cp("dve", OT[eb][1][:], PS[ob][:], [("ps", ob)], [("Ot", eb, 1)])
                dma(LM[eb][0:64, :], OT[eb][0][64:128, :], [("Ot", eb, 0)], [("Lm", eb)])
                dma(LM[eb][64:128, :], OT[eb][1][0:64, :], [("Ot", eb, 1)], [("Lm", eb)])

                def later():
                    stt(LM[eb][:], EE[eb][:], 1.0, LM[eb][:], ALU.add, ALU.mult,
                        [("E", eb), ("Lm", eb)], [("Lm", eb)])
                    if I == NB - 1:
                        act(LM[eb][:], LM[eb][:], AF.Ln, [("Lm", eb)], [("Lm", eb)])
                        act(LM[eb][:], LM[eb][:], AF.Exp, [("Lm", eb)], [("Lm", eb)], scale=-1.0)
                        tail_next.append(later2)
                        return
                    A("dve", lambda e: e.reciprocal(out=LM[eb][:], in_=LM[eb][:]), [("Lm", eb)], [("Lm", eb)])
                    later2()

                def later2():
                    em = "dve" if I == NB - 1 else "pool"
                    tt(em, LM[eb][:], LM[eb][:], GG[eb][:], ALU.mult, [("Lm", eb), ("G", eb)], [("Lm", eb)])
                    tt(em, YS[eb][0:64, :], OT[eb][0][0:64, :], LM[eb][0:64, :], ALU.mult,
                       [("Ot", eb, 0), ("Lm", eb)], [("ys", eb)])
                    tt(em, YS[eb][64:128, :], OT[eb][1][64:128, :], LM[eb][64:128, :], ALU.mult,
                       [("Ot", eb, 1), ("Lm", eb)], [("ys", eb)])
                    dma(ysc_d[chunk, :, bslice(I)], YS[eb][:], [("ys", eb)], [("ysc", chunk, I)])
                deferred.append((pu + 4, later))

            def emit_rest(pu):
                I, kb = pus[pu]
                jd = kb - 4 * I
                qlo = 128 * max(0, jd)
                N = 512 - qlo
                j = pu % 2
                pt = pu % 3
                for hd in (0, 1):
                    act(PT[pt][:, hd * 512: hd * 512 + N], PS[2 * j + hd][:, qlo:512], AF.Exp,
                        [("ps", 2 * j + hd)], [("PT", pt, hd)])
                lastkb = (kb == 4 * (I + 1) - 1)
                for hd in (0, 1):
                    ok = obank(I, hd)
                    lhsT = VV[:, kb, 0:128] if hd == 0 else VV[:, kb, 64:192]
                    mm(PS[ok][:, qlo:512], lhsT, PT[pt][:, hd * 512: hd * 512 + N], kb == 0, lastkb,
                       [("PT", pt, hd), ("VV", kb // 4)], [("ps", ok)])
                if lastkb:
                    epilogue(I, pu)

            emit_S(0)
            for pu in range(n):
                if pu + 1 < n:
                    emit_S(pu + 1)
                if extra is not None:
                    extra(pu, deferred)
                emit_rest(pu)
                keep = []
                for due, fn in deferred:
                    if due <= pu:
                        fn()
                    else:
                        keep.append((due, fn))
                deferred[:] = keep
            tail_fns.extend(fn for due, fn in deferred)
            if prefetch is not None and _nopf:
                prefetch()

        tail_fns = []
        tail_next = []

        def flush_tail():
            fns = list(tail_fns)
            del tail_fns[:]
            for fn in fns:
                fn()
            tail_fns.extend(tail_next)
            del tail_next[:]

        def v_evac(pbk, g):
            pv = PS[pbk][:].rearrange("p (j c) -> p j c", c=128)
            cp("dve", VV[:, 4 * g:4 * g + 4, 0:64], pv[:, :, 0:64], [("ps", pbk)], [("VV", g)])
            act(VV[:, 4 * g:4 * g + 4, 128:192], pv[:, :, 64:128], AF.Copy, [("ps", pbk)],
                [("VV", g)] + ([("ps", pbk)] if _ordrr else []))

        try:
          ada_units = [(l, cb) for l in range(1, nl) for cb in range(6)]
          per_blk = (len(ada_units) + NB - 1) // NB

          def after_block(b):
              for _ in range(per_blk):
                  if ada_units:
                      l_, cb_ = ada_units.pop(0)
                      ada_unit(l_, cb_)
                      if cb_ == 5:
                          ada_finish(l_)

          end_phase(-1, xT_in, None, after_block)
          while ada_units:
              after_block(0)
          sch.alias(["wst"], ["qn", "kvn", "kr"])
          stop_at("pro")
          for l in range(nl):
              sch.alias(["xo", "yrb"], ["QT", "KT", "VV"])
              mset("pool", VV[:, :, 64:128], 1.0, [("VV", g) for g in range(8)])
              WC_ql = WC[:, 0:2048].rearrange("p (k c) -> p k c", c=256)
              WC_kvl = WC[:, 2048:3072].rearrange("p (k c) -> p k c", c=128)
              WC_kr = WC[:, 3072:3328].rearrange("p (k c) -> p k c", c=32)
              WC_krr = WC[:, 3328:3584].rearrange("p (k c) -> p k c", c=32)
              WC_ff = WC[:, 3584:4160].rearrange("p (k c) -> p k c", c=72)

              def load_wc(ll, WC_krr=WC_krr):
                  dma(WC[:], wcom_d[ll], (), [("WC",)], q="pool", max_dma_last_dim=4096)
                  ts("pool", WC_krr[:, :, 0:16], WC_krr[:, :, 0:16], -1.0, None, ALU.mult, None,
                     [("WC",)], [("WC",)])

              if l == 0:
                  load_wc(0)
              for b in range(NB):
                  dma(RC[0:96, :], rC_d[:, bslice(b)], [("ropeC", b)], [("rc",)])
                  dma(RS[0:96, :], rS_d[:, bslice(b)], [("ropeS", b)], [("rsn",)])
                  pq = []
                  for c in range(2):
                      pbk = next_pb()
                      pq.append(pbk)
                      for k in range(8):
                          mm(PS[pbk][:], WC_ql[:, k, c * 128:(c + 1) * 128], hT[:, k, bslice(b)], k == 0, k == 7,
                             [("WC",), ("hT", k, b)], [("ps", pbk)])
                      cp("dve", T[c][:], PS[pbk][:], [("ps", pbk)], [("T", c)])
                      act(SQ[c][:], T[c][:], AF.Square, [("T", c)], [("sq", c)])
                  pkv = next_pb()
                  for k in range(8):
                      mm(PS[pkv][:], WC_kvl[:, k, :], hT[:, k, bslice(b)], k == 0, k == 7,
                         [("WC",), ("hT", k, b)], [("ps", pkv)])
                  cp("dve", T[2][:], PS[pkv][:], [("ps", pkv)], [("T", 2)])
                  p1 = next_pb()
                  for k in range(8):
                      mm(PS[p1][0:32, :], WC_kr[:, k, :], hT[:, k, bslice(b)], k == 0, k == 7,
                         [("WC",), ("hT", k, b)], [("ps", p1)])
                  tt("dve", T[3][0:32, :], PS[p1][0:32, :], RC[0:32, :], ALU.mult, [("ps", p1), ("rc",)], [("T", 3)])
                  p2 = next_pb()
                  for k in range(8):
                      mm(PS[p2][0:32, :], WC_krr[:, k, :], hT[:, k, bslice(b)], k == 0, k == 7,
                         [("WC",), ("hT", k, b)], [("ps", p2)])
                  tt("dve", T[4][0:32, :], PS[p2][0:32, :], RS[0:32, :], ALU.mult, [("ps", p2), ("rsn",)], [("T", 4)])
                  tt("pool", KR[0:32, bslice(b)], T[3][0:32, :], T[4][0:32, :], ALU.add,
                     [("T", 3), ("T", 4)], [("kr", b)])
                  for c in range(2):
                      mm(PS[MB][:], ONESB[:], SQ[c][:], c == 0, c == 1, [("onesb",), ("sq", c)], [("ps", MB)])
                  act(T[6][:], PS[MB][:], AF.Ln, [("ps", MB)], [("T", 6)], bias=EPS, scale=1.0 / 256)
                  act(T[6][:], T[6][:], AF.Exp, [("T", 6)], [("T", 6)], scale=-0.5)
                  for c in range(2):
                      stt(QN[:, c, bslice(b)], T[c][:], col("qg", l * 2 + c), T[6][:], ALU.mult, ALU.mult,
                          [("T", c), ("cols",), ("T", 6)], [("qn", b)])
                  act(SQ[0][:], T[2][:], AF.Square, [("T", 2)], [("sq", 0)])
                  mm(PS[MB2][:], ONESB[:], SQ[0][:], True, True, [("onesb",), ("sq", 0)], [("ps", MB2)])
                  act(T[5][:], PS[MB2][:], AF.Ln, [("ps", MB2)], [("T", 5)], bias=EPS, scale=1.0 / 128)
                  act(T[5][:], T[5][:], AF.Exp, [("T", 5)], [("T", 5)], scale=-0.5)
                  stt(KVN[:, bslice(b)], T[2][:], col("kvg", l), T[5][:], ALU.mult, ALU.mult,
                      [("T", 2), ("cols",), ("T", 5)], [("kvn", b)])

              stop_at("mlacom")
              def load_mla(p, l=l):
                  dma(WP[:, 0:W_MLA], wmla_d[l, p], (), [("WP",)], q="pool", max_dma_last_dim=4096)

              def load_fox(p, l=l):
                  dma(WP[:, 0:W_FOX], wfox_d[l, p], (), [("WP",)], q="pool", max_dma_last_dim=4096)

              load_mla(0)
              for p in range(4):
                  uq = [WP[:, h * 384: h * 384 + 192].rearrange("p (k c) -> p k c", c=96) for h in range(2)]
                  uqr = [WP[:, h * 384 + 192: h * 384 + 384].rearrange("p (k c) -> p k c", c=96) for h in range(2)]
                  uk = [WP[:, 768 + h * 96: 768 + (h + 1) * 96] for h in range(2)]
                  uv = WP[:, 960:1088]
                  wg = WP[:, 1088:2112].rearrange("p (k c) -> p k c", c=128)
                  for h in range(2):
                      act(uqr[h][:, :, 64:80], uqr[h][:, :, 64:80], AF.Copy, [("WP",)], [("WP",)], scale=-1.0)
                  for b in range(NB):
                      dma(RC[0:96, :], rC_d[:, bslice(b)], [("ropeC", b)], [("rc",)])
                      dma(RS[0:96, :], rS_d[:, bslice(b)], [("ropeS", b)], [("rsn",)])
                      for h in range(2):
                          p1 = next_pb()
                          for k in range(2):
                              mm(PS[p1][0:96, :], uq[h][:, k, :], QN[:, k, bslice(b)], k == 0, k == 1,
                                 [("WP",), ("qn", b)], [("ps", p1)])
                          tA, kA = (T[3], ("T", 3)) if h == 0 else (OT[0][0], ("Ot", 0, 0))
                          tB, kB = (T[4], ("T", 4)) if h == 0 else (OT[0][1], ("Ot", 0, 1))
                          tt("dve", tA[64:96, :], PS[p1][64:96, :], RC[64:96, :], ALU.mult,
                             [("ps", p1), ("rc",)], [kA])
                          act(QT[h][0:64, bslice(b)], PS[p1][0:64, :], AF.Copy, [("ps", p1)],
                              [("QT", h, b), ("ps", p1)], scale=MLA_SCALE)
                          p2 = next_pb()
                          for k in range(2):
                              mm(PS[p2][0:96, :], uqr[h][:, k, :], QN[:, k, bslice(b)], k == 0, k == 1,
                                 [("WP",), ("qn", b)], [("ps", p2)])
                          tt("dve", tB[64:96, :], PS[p2][64:96, :], RS[64:96, :], ALU.mult,
                             [("ps", p2), ("rsn",)], [kB])
                          tt("pool", QT[h][64:96, bslice(b)], tA[64:96, :], tB[64:96, :], ALU.add,
                             [kA, kB], [("QT", h, b)])
                          pk = next_pb()
                          mm(PS[pk][0:96, :], uk[h], KVN[:, bslice(b)], True, False,
                             [("WP",), ("kvn", b)], [("ps", pk)])
                          mm(PS[pk][0:96, :], selr, KR[0:32, bslice(b)], False, True,
                             [("cbf",), ("kr", b)], [("ps", pk)])
                          act(KT[h][0:96, bslice(b)], PS[pk][0:96, :], AF.Copy, [("ps", pk)], [("KT", h, b)])
                      pv = next_pb()
                      for j in range(4):
                          t = 4 * b + j
                          mm(PS[pv][:, j * 128:(j + 1) * 128], KVN[:, t * 128:(t + 1) * 128], uv, True, True,
                             [("WP",), ("kvn", b)], [("ps", pv)], skip=True)
                      v_evac(pv, b)
                      if b in (1, 2):
                          flush_tail()
                  stop_at("mlaproj")
                  extra = None
                  if p == 3:
                      sch.alias(["kvn"], ["C"])
                      sch.alias(["qn"], ["wout"])
                      dma(WOUT, wout_d[l].rearrange("p (k c) -> p k c", c=1024), (), [("wout",)], q="pool",
                          max_dma_last_dim=4096)
                      mset("pool", CC[0:32, :], 1.0, [("C", b) for b in range(NB)])
                      mset("dve", T[5][:], 1.0, [("T", 5)])

                      def fox_common(b, deferred, pu, l=l, WC_ff=WC_ff):
                          for k in range(8):
                              mm(PS[GB][0:72, :], WC_ff[:, k, :], hT[:, k, bslice(b)], k == 0, k == 7,
                                 [("WC",), ("hT", k, b)], [("ps", GB)])

                          def later():
                              act(T[0][0:72, :], PS[GB][0:72, :], AF.Exp, [("ps", GB), ("nbf",)],
                                  [("T", 0), ("ps", GB)], bias=NBF[0:72, l:l + 1], scale=-1.0)
                              act(T[0][0:72, :], T[0][0:72, :], AF.Ln, [("T", 0)], [("T", 0)], bias=1.0, scale=1.0)
                              cur, prv = T[1 + b % 2], T[1 + (b + 1) % 2]
                              kc, kp = ("T", 1 + b % 2), ("T", 1 + (b + 1) % 2)
                              init = 0.0 if b == 0 else prv[0:72, 511:512]
                              A("dve", lambda e, cur=cur, init=init: e.tensor_tensor_scan(
                                  out=cur[0:72, :], data0=T[5][0:72, :], data1=T[0][0:72, :], initial=init,
                                  op0=ALU.mult, op1=ALU.add), [("T", 5), ("T", 0), kp], [kc])
                              cp("pool", HB[0][0:72, :], cur[0:72, :], [kc], [("sq", 0)])
                              cp("pool", CC[0:8, bslice(b)], HB[0][0:8, :], [("sq", 0)], [("C", b)])
                              tt("pool", T[3][0:72, :], cur[0:72, :], HB[0][0:72, :], ALU.subtract,
                                 [kc, ("sq", 0)], [("T", 3)])
                              cp("pool", HB[1][0:72, :], T[3][0:72, :], [("T", 3)], [("sq", 1)])
                              cp("pool", CC[32:40, bslice(b)], HB[1][32:40, :], [("sq", 1)], [("C", b)])
                              tt("pool", T[4][0:72, :], T[3][0:72, :], HB[1][0:72, :], ALU.subtract,
                                 [("T", 3), ("sq", 1)], [("T", 4)])
                              cp("pool", CC[64:72, bslice(b)], T[4][64:72, :], [("T", 4)], [("C", b)])
                          deferred.append((pu + 1, later))

                      fc_sched = {6: 0, 14: 1, 26: 2, 42: 3, 62: 4, 86: 5, 114: 6, 122: 7}

                      def extra(pu, deferred):
                          if pu in fc_sched:
                              fox_common(fc_sched[pu], deferred, pu)
                  attention_pair((96, 96), maskM, 4 + p, wg,
                                 prefetch=(lambda p=p: load_mla(p + 1)) if p < 3 else (lambda: load_fox(0)),
                                 extra=extra)
                  if p == 3 and l + 1 < nl:
                      load_wc(l + 1)

              stop_at("mla")
              stop_at("foxcom")
              for p in range(4):
                  wq = WP[:, 0:1024].rearrange("p (k c) -> p k c", c=128)
                  wk = WP[:, 1024:2048].rearrange("p (k c) -> p k c", c=128)
                  wv = WP[:, 2048:3072].rearrange("p (k c) -> p k c", c=128)
                  wg = WP[:, 3072:4096].rearrange("p (k c) -> p k c", c=128)
                  for b in range(NB):
                      if p == 0:
                          mset("pool", QT[1][0:64, bslice(b)], 0.0, [("QT", 1, b)])
                          mset("pool", KT[1][0:64, bslice(b)], 0.0, [("KT", 1, b)])
                      for (wt, sel, dst, nm, sc) in ((wq, selq[p], QT, "QT", 0.125), (wk, selk[p], KT, "KT", None)):
                          pm = next_pb()
                          for k in range(8):
                              mm(PS[pm][:], wt[:, k, :], hT[:, k, bslice(b)], k == 0, k == 7,
                                 [("WP",), ("hT", k, b)], [("ps", pm)])
                          pa = next_pb()
                          mm(PS[pa][0:70, :], sel, CC[0:72, bslice(b)], True, True,
                             [("cbf",), ("C", b)], [("ps", pa)])
                          if sc is not None:
                              ts("dve", dst[0][0:64, bslice(b)], PS[pm][0:64, :], sc, None, ALU.mult, None,
                                 [("ps", pm)], [(nm, 0, b)])
                              ts("dve", dst[1][64:128, bslice(b)], PS[pm][64:128, :], sc, None, ALU.mult, None,
                                 [("ps", pm)], [(nm, 1, b)])
                              ts("dve", dst[0][64:70, bslice(b)], PS[pa][64:70, :], sc, None, ALU.mult, None,
                                 [("ps", pa)], [(nm, 0, b)])
                              ts("dve", dst[1][0:6, bslice(b)], PS[pa][0:6, :], sc, None, ALU.mult, None,
                                 [("ps", pa)], [(nm, 1, b)])
                          else:
                              act(dst[0][0:64, bslice(b)], PS[pm][0:64, :], AF.Copy, [("ps", pm)], [(nm, 0, b)])
                              act(dst[1][64:128, bslice(b)], PS[pm][64:128, :], AF.Copy, [("ps", pm)], [(nm, 1, b)])
                              act(dst[0][64:70, bslice(b)], PS[pa][64:70, :], AF.Copy, [("ps", pa)], [(nm, 0, b)])
                              act(dst[1][0:6, bslice(b)], PS[pa][0:6, :], AF.Copy, [("ps", pa)], [(nm, 1, b)])
                      pv = next_pb()
                      for j in range(4):
                          t = 4 * b + j
                          for k in range(8):
                              mm(PS[pv][:, j * 128:(j + 1) * 128], hT[:, k, t * 128:(t + 1) * 128], wv[:, k, :],
                                 k == 0, k == 7, [("WP",), ("hT", k, b)], [("ps", pv)], skip=True)
                      v_evac(pv, b)
                      if b in (1, 2):
                          flush_tail()
                  stop_at(f"foxproj{p}")
                  attention_pair((70, 128), maskF, p, wg, prefetch=(lambda p=p: load_fox(p + 1)) if p < 3 else None)
                  stop_at(f"foxatt{p}")

              stop_at("fox")
              sch.alias(["QT", "KT", "VV"], ["xo", "yrb"])
              end_phase(l, xT_in if l == 0 else xs_d, None if l == 0 else "xs")
              sch.alias(["C"], ["kvn"])
              sch.alias(["wout"], ["qn"])

        except _Stop:
            pass

        A("sp", None, [("out", d, b) for d in range(8) for b in range(NB)], ())

        sem_eng = {e: es.enter_context(nc.semaphore(f"s_{e}")) for e in ("pe", "act", "dve", "pool")}
        dma_sems = {
            "sp": [es.enter_context(nc.semaphore(f"d_sp{i}")) for i in range(16)],
            "pool": [es.enter_context(nc.semaphore(f"d_pl{i}")) for i in range(6)],
        }
        sch.finalize(sem_eng, dma_sems)
        block = es.enter_context(nc.Block())

        @block.tensor
        def _(e):
            sch.emit("pe", e)

        @block.scalar
        def _(e):
            sch.emit("act", e)

        @block.vector
        def _(e):
            sch.emit("dve", e)

        @block.gpsimd
        def _(e):
            sch.emit("pool", e)

        @block.sync
        def _(e):
            sch.emit("sp", e)

    return nc


def _consts_bf():
    cb = np.zeros((128, CB_N), np.float32)
    cb[:, CB_ID:CB_ID + 128] = np.eye(128, dtype=np.float32)
    k = np.arange(128)[:, None]
    q = np.arange(128)[None, :]
    cb[:, CB_MF:CB_MF + 128] = np.where(k <= q, 0.0, -30000.0)
    cb[:, CB_MM:CB_MM + 128] = np.where((k // 64) <= (q // 64), 0.0, -30000.0)
    for p in range(4):
        ha, hb = 2 * p, 2 * p + 1
        sq = np.zeros((128, 70), np.float32)
        sk = np.zeros((128, 70), np.float32)
        for j, g in enumerate((0, 32, 64)):
            sq[g + ha, 64 + j] = -8.0
            sq[g + hb, 0 + j] = -8.0
            sk[g + ha, 67 + j] = 1.0
            sk[g + hb, 3 + j] = 1.0
        sq[8, 67:70] = 8.0
        sq[8, 3:6] = 8.0
        sk[8, 64:67] = 1.0
        sk[8, 0:3] = 1.0
        cb[:, CB_SQ + p * 70: CB_SQ + (p + 1) * 70] = sq
        cb[:, CB_SK + p * 70: CB_SK + (p + 1) * 70] = sk
    sr = np.zeros((128, 96), np.float32)
    for r in range(32):
        sr[r, 64 + r] = 1.0
    cb[:, CB_SR:CB_SR + 96] = sr
    return cb


def _kchunks(w):
    kk = w.shape[0] // 128
    return w.reshape(kk, 128, w.shape[1]).transpose(1, 0, 2)


def _pack_weights(layers, w_in, w_uq, w_ukv, w_out, w_ada, b_ada):
    nl = len(layers)
    wfox = np.zeros((nl, 4, 128, W_FOX), np.float32)
    wmla = np.zeros((nl, 4, 128, W_MLA), np.float32)
    wcom = np.zeros((nl, 128, W_COM), np.float32)
    wout = np.zeros((nl, 128, 8192), np.float32)
    wada = np.zeros((nl, 6, 128, 4096), np.float32)
    bada = np.zeros((nl, 6, 1, 512), np.float32)
    for i, l in enumerate(layers):
        wi = _kchunks(w_in[l])
        uqc = _kchunks(w_uq[l])
        ukv = w_ukv[l]
        for p in range(4):
            buf = np.zeros((128, W_FOX), np.float32)
            buf[:, 0:1024] = wi[:, :, p * 128:(p + 1) * 128].reshape(128, 1024)
            buf[:, 1024:2048] = wi[:, :, 512 + p * 128: 512 + (p + 1) * 128].reshape(128, 1024)
            buf[:, 2048:3072] = wi[:, :, 1024 + p * 128: 1024 + (p + 1) * 128].reshape(128, 1024)
            buf[:, 3072:4096] = wi[:, :, 1544 + p * 128: 1544 + (p + 1) * 128].reshape(128, 1024)
            wfox[i, p] = buf
            buf = np.zeros((128, W_MLA), np.float32)
            for h in range(2):
                hd = 2 * p + h
                buf[:, h * 384: h * 384 + 192] = uqc[:, :, hd * 96:(hd + 1) * 96].reshape(128, 192)
                t = np.zeros((128, 2, 96), np.float32)
                t[:, :, 64:80] = uqc[:, :, hd * 96 + 80: hd * 96 + 96]
                t[:, :, 80:96] = uqc[:, :, hd * 96 + 64: hd * 96 + 80]
                buf[:, h * 384 + 192: h * 384 + 384] = t.reshape(128, 192)
                buf[:, 768 + h * 96: 768 + h * 96 + 64] = ukv[:, hd * 128: hd * 128 + 64]
                buf[:, 960 + h * 64: 960 + (h + 1) * 64] = ukv[:, hd * 128 + 64: hd * 128 + 128]
            buf[:, 1088:2112] = wi[:, :, 2472 + p * 128: 2472 + (p + 1) * 128].reshape(128, 1024)
            wmla[i, p] = buf
        buf = np.zeros((128, W_COM), np.float32)
        buf[:, 0:2048] = wi[:, :, 2056:2312].reshape(128, 2048)
        buf[:, 2048:3072] = wi[:, :, 2312:2440].reshape(128, 1024)
        buf[:, 3072:3328] = wi[:, :, 2440:2472].reshape(128, 256)
        t = np.zeros((128, 8, 32), np.float32)
        t[:, :, 0:16] = wi[:, :, 2456:2472]
        t[:, :, 16:32] = wi[:, :, 2440:2456]
        buf[:, 3328:3584] = t.reshape(128, 256)
        t = np.zeros((128, 8, 72), np.float32)
        for g in (0, 32, 64):
            t[:, :, g:g + 8] = wi[:, :, 1536:1544]
        buf[:, 3584:4160] = t.reshape(128, 576)
        wcom[i] = buf
        wout[i] = _kchunks(w_out[l]).reshape(128, 8192)
        wa = _kchunks(w_ada[l])
        for cb in range(6):
            wada[i, cb] = wa[:, :, cb * 512:(cb + 1) * 512].reshape(128, 4096)
            bada[i, cb, 0] = b_ada[l][cb * 512:(cb + 1) * 512]
    return wfox, wmla, wcom, wout, wada, bada


def _pack_cols(layers, b, c, norm_g, final_g, q_norm_g, kv_norm_g, b_f):
    nl = len(layers)
    CL = _col_layout(nl)
    cols = np.zeros((128, CL["_n"]), np.float32)
    for i, l in enumerate(layers):
        cols[:, CL["ng"] + i * 8: CL["ng"] + (i + 1) * 8] = norm_g[l].reshape(8, 128).T
        cols[:, CL["qg"] + i * 2: CL["qg"] + (i + 1) * 2] = q_norm_g[l].reshape(2, 128).T
        cols[:, CL["kvg"] + i] = kv_norm_g[l]
        for g in (0, 32, 64):
            cols[g:g + 8, CL["bf"] + i] = b_f[l]
    cols[:, CL["fg"]:CL["fg"] + 8] = final_g.reshape(8, 128).T
    inv_freq = (1.0 / (10000.0 ** (np.arange(0, 32, 2, dtype=np.float32) / np.float32(32)))).astype(np.float32)
    for base in (0, 64):
        cols[base:base + 16, CL["invf"]] = inv_freq
        cols[base + 16:base + 32, CL["invf"]] = inv_freq
    cols[0:64, CL["scl"]] = 1.0
    cols[64:128, CL["scl"]] = MLA_SCALE
    cols[:, CL["cc"]:CL["cc"] + 8] = c[b].reshape(8, 128).T
    return cols


_PROG_CACHE = {}


def _program(nl, last):
    key = (nl, last)
    if key not in _PROG_CACHE:
        _PROG_CACHE[key] = build_program(nl, last)
    return _PROG_CACHE[key]


def _launch(layers, last, xT_list, inputs, cbf):
    nl = len(layers)
    wfox, wmla, wcom, wout, wada, bada = _pack_weights(
        layers, inputs["w_in"], inputs["w_uq"], inputs["w_ukv"], inputs["w_out"], inputs["w_ada"],
        inputs["b_ada"])
    in_maps = []
    for b in range(8):
        in_maps.append({
            "xT": xT_list[b],
            "pos": np.ascontiguousarray(inputs["positions"][b].reshape(1, S)).astype(np.int32),
            "cols": _pack_cols(layers, b, inputs["c"], inputs["norm_g"], inputs["final_g"],
                               inputs["q_norm_g"], inputs["kv_norm_g"], inputs["b_f"]),
            "cbf": cbf, "wfox": wfox, "wmla": wmla, "wcom": wcom, "wout": wout, "wada": wada, "bada": bada,
        })
    nc = _program(nl, last)
    res = run_bass_kernel_spmd(nc, in_maps, core_ids=list(range(8)))
    return [np.asarray(r["out"]) for r in res.results]


LAYER_GROUPS = [[0, 1, 2, 3]]


def kernel(**inputs):
    inputs = {k: np.asarray(v) for k, v in inputs.items()}
    x = inputs["x"].astype(np.float32, copy=False)
    cbf = _consts_bf()
    xT = [np.ascontiguousarray(x[b].T) for b in range(8)]
    for gi, layers in enumerate(LAYER_GROUPS):
        xT = _launch(layers, gi == len(LAYER_GROUPS) - 1, xT, inputs, cbf)
    out = np.stack([np.ascontiguousarray(xT[b].T) for b in range(8)], axis=0)
    return out.astype(np.float32, copy=False)
```

```python
import numpy as np
from contextlib import ExitStack
import concourse.bass as bass
import concourse.mybir as mybir
from concourse.bass_utils import run_bass_kernel_spmd

F32 = mybir.dt.float32
BF16 = mybir.dt.bfloat16
I32 = mybir.dt.int32
AF = mybir.ActivationFunctionType
ALU = mybir.AluOpType

D = 1024
S = 4096
NB = 8
BLK = 512
EPS = 1e-6
MLA_SCALE = float(96 ** -0.5)
W_FOX = 4096
W_MLA = 2112
W_COM = 4160
TWO_PI = float(2 * np.pi)
C1 = 6.28125
C2 = float(2 * np.pi - 6.28125)

def _col_layout(nl):
    off = {}
    o = 0
    for name, n in (("ng", nl * 8), ("fg", 8), ("qg", nl * 2), ("kvg", nl), ("bf", nl),
                    ("invf", 1), ("scl", 1), ("cc", 8)):
        off[name] = o
        o += n
    off["_n"] = o
    return off

CB_ID, CB_MF, CB_MM, CB_SQ, CB_SK, CB_SR, CB_N = 0, 128, 256, 384, 664, 944, 1040


class _Op:
    __slots__ = ("eng", "fn", "deps", "idx", "needed", "dma", "sem", "val", "prev")


class Sched:
    ENGS = ("pe", "act", "dve", "pool", "sp")

    def __init__(self):
        self.streams = {e: [] for e in self.ENGS}
        self.keys = {}
        self.seeds = {}

    def _state(self, key):
        st = self.keys.get(key)
        if st is None:
            st = [list(self.seeds.get(key[0], [])), []]
            self.keys[key] = st
        return st

    @staticmethod
    def _reduce(deps, op):
        best = {}
        dmas = {}
        for d in deps:
            if d is op:
                continue
            if d.dma:
                dmas[id(d)] = d
            else:
                b = best.get(d.eng)
                if b is None or d.idx > b.idx:
                    best[d.eng] = d
        return list(best.values()) + list(dmas.values())

    def add(self, eng, fn, reads=(), writes=(), dma=False):
        op = _Op()
        op.eng, op.fn, op.dma, op.needed = eng, fn, dma, False
        op.sem = op.val = op.prev = None
        op.idx = len(self.streams[eng])
        deps = []
        for k in reads:
            deps.extend(self._state(k)[0])
        for k in writes:
            st = self._state(k)
            deps.extend(st[0])
            deps.extend(st[1])
        op.deps = self._reduce(deps, op)
        for k in reads:
            st = self._state(k)
            if op.dma:
                st[1].append(op)
            else:
                st[1] = [r for r in st[1] if r.dma or r.eng != eng] + [op]
        for k in writes:
            st = self._state(k)
            st[0] = [op]
            st[1] = []
        self.streams[eng].append(op)
        return op

    def alias(self, from_names, to_names):
        deps = []
        for k in list(self.keys):
            if k[0] in from_names:
                st = self.keys.pop(k)
                deps.extend(st[0])
                deps.extend(st[1])
        for n in from_names:
            deps.extend(self.seeds.pop(n, []))
        deps = self._reduce(deps, None)
        for n in to_names:
            self.seeds[n] = deps

    @staticmethod
    def _needs_wait(op, d):
        if d.dma:
            return True
        if d.eng == op.eng:
            if op.eng == "pe":
                return False
            if op.idx - d.idx > 3:
                return False
        return True

    def finalize(self, sem_eng, dma_sems):
        for ops in self.streams.values():
            for op in ops:
                if op.dma:
                    op.needed = True
                for d in op.deps:
                    if self._needs_wait(op, d):
                        d.needed = True
        for e, ops in self.streams.items():
            cnt = 0
            k = 0
            pool = dma_sems.get(e, [])
            hist = []
            for op in ops:
                if not op.needed:
                    continue
                if op.dma:
                    n = len(pool)
                    op.sem = pool[k % n]
                    op.val = 16 * (k // n + 1)
                    op.prev = hist[k - n] if k >= n else None
                    hist.append(op)
                    k += 1
                else:
                    cnt += 1
                    op.sem = sem_eng[e]
                    op.val = cnt

    def emit(self, eng, E):
        seen = {}
        seen_dma = set()
        nwait = 0
        for op in self.streams[eng]:
            for d in op.deps:
                if not self._needs_wait(op, d):
                    continue
                if d.dma:
                    if id(d) in seen_dma:
                        continue
                    E.wait_ge(d.sem, d.val)
                    seen_dma.add(id(d))
                    nwait += 1
                else:
                    if seen.get(d.eng, 0) >= d.val:
                        continue
                    E.wait_ge(d.sem, d.val)
                    seen[d.eng] = d.val
                    nwait += 1
            if op.dma and op.needed and op.prev is not None and id(op.prev) not in seen_dma:
                E.wait_ge(op.prev.sem, op.prev.val)
                seen_dma.add(id(op.prev))
                nwait += 1
            if op.fn is None:
                continue
            inst = op.fn(E)
            if op.needed:
                inst.then_inc(op.sem, 16 if op.dma else 1)
        return nwait


def build_program(nl, last):
    nc = bass.Bass("TRN2", target_bir_lowering=False)
    CL = _col_layout(nl)

    def din(name, shape, dt=F32):
        return nc.dram_tensor(name, shape, dt, kind="ExternalInput").ap()

    xT_in = din("xT", [D, S])
    pos_d = din("pos", [1, S], I32)
    cols_d = din("cols", [128, CL["_n"]])
    cbf_d = din("cbf", [128, CB_N])
    wfox_d = din("wfox", [nl, 4, 128, W_FOX])
    wmla_d = din("wmla", [nl, 4, 128, W_MLA])
    wcom_d = din("wcom", [nl, 128, W_COM])
    wout_d = din("wout", [nl, 128, 8192])
    wada_d = din("wada", [nl, 6, 128, 4096])
    bada_d = din("bada", [nl, 6, 1, 512])
    out_d = nc.dram_tensor("out", [D, S], F32, kind="ExternalOutput").ap()
    xs_d = nc.dram_tensor("xs_scr", [D, S], F32, kind="Internal").ap()
    ysc_d = nc.dram_tensor("ysc_scr", [8, 128, S], BF16, kind="Internal").ap()
    rC_d = nc.dram_tensor("ropec_scr", [96, S], F32, kind="Internal").ap()
    rS_d = nc.dram_tensor("ropes_scr", [96, S], F32, kind="Internal").ap()

    sch = Sched()
    A = sch.add
    import os as _os
    _kstop = _os.environ.get("KSTOP", "")
    _nopf = _os.environ.get("K_NOPF", "") == "1"
    _ordrr = True

    class _Stop(Exception):
        pass

    def stop_at(tag):
        if _kstop == tag:
            raise _Stop()

    with ExitStack() as es:
        def sb(name, shape, dt):
            return es.enter_context(nc.sbuf_tensor(name, shape, dt))

        hT_t = sb("hT", [128, 8 * S], BF16)
        hT = hT_t[:].rearrange("p (k t) -> p k t", t=S)
        R1 = sb("R1", [128, 22528], BF16)
        R2 = sb("R2", [128, 16384], BF16)
        WP = sb("WP", [128, W_FOX], BF16)
        WC = sb("WC", [128, W_COM], BF16)
        CBF = sb("CBF", [128, CB_N], BF16)
        COLS = sb("COLS", [128, CL["_n"]], F32)
        MODT = sb("MODT", [128, nl * 24], F32)
        GP1 = sb("GP1", [128, nl * 8], F32)
        NBF = sb("NBF", [128, nl], F32)
        CSIL = sb("CSIL", [128, 8], F32)
        ONESB = sb("ONESB", [128, 128], BF16)
        ONE1 = sb("ONE1", [1, 1], F32)
        NT = 7
        T = [sb(f"T{i}", [128, 512], F32) for i in range(NT)]
        SQ = [sb(f"SQ{i}", [128, 512], BF16) for i in range(2)]
        HB = SQ
        PT = [sb(f"PT{i}", [128, 1024], BF16) for i in range(3)]
        OT = [[sb(f"OT{i}{h}", [128, 512], F32) for h in range(2)] for i in range(2)]
        LM = [sb(f"LM{i}", [128, 512], F32) for i in range(2)]
        EE = [sb(f"EE{i}", [128, 512], F32) for i in range(2)]
        GG = [sb(f"GG{i}", [128, 512], BF16) for i in range(2)]
        YS = [sb(f"YS{i}", [128, 512], BF16) for i in range(2)]
        RC = sb("RC", [128, 512], F32)
        RS = sb("RS", [128, 512], F32)
        PSA = [es.enter_context(nc.psum_tensor(f"psA{i}", [128, 1024], F32)) for i in range(2)]
        PSS = [es.enter_context(nc.psum_tensor(f"ps{i}", [128, 512], F32)) for i in range(4, 8)]
        PS = [PSA[i // 2][:, (i % 2) * 512:(i % 2 + 1) * 512] for i in range(4)] + [t[:] for t in PSS]

        QT = [R1[:, i * S:(i + 1) * S] for i in range(2)]
        KT = [R1[:, 2 * S + i * S: 2 * S + (i + 1) * S] for i in range(2)]
        VV = R1[:, 4 * S: 4 * S + 32 * 192].rearrange("p (t c) -> p t c", c=192)
        NXO = 12
        XOS = [R1[:, i * 1024:(i + 1) * 1024].bitcast(F32) for i in range(NXO)]
        YRB = [R1[:, 12288 + i * 4096: 12288 + (i + 1) * 4096].rearrange("p (c t) -> p c t", t=512)
               for i in range(2)]
        QN = R2[:, 0:8192].rearrange("p (k t) -> p k t", t=S)
        KVN = R2[:, 8192:12288]
        KR = R2[:, 12288:16384]
        WOUT = R2[:, 0:8192].rearrange("p (k c) -> p k c", c=1024)
        CC = R2[:, 8192:12288]
        WST = [R2[:, i * 8192:(i + 1) * 8192].bitcast(F32).rearrange("p (k c) -> p k c", c=512)
               for i in range(2)]
        ident = CBF[:, CB_ID:CB_ID + 128]
        maskF = CBF[:, CB_MF:CB_MF + 128]
        maskM = CBF[:, CB_MM:CB_MM + 128]
        selq = [CBF[0:72, CB_SQ + p * 70: CB_SQ + (p + 1) * 70] for p in range(4)]
        selk = [CBF[0:72, CB_SK + p * 70: CB_SK + (p + 1) * 70] for p in range(4)]
        selr = CBF[0:32, CB_SR:CB_SR + 96]

        def col(name, i=0, rows=128):
            o = CL[name] + i
            return COLS[0:rows, o:o + 1]

        MB, MB2, GB = 6, 7, 7
        PBS = [0, 1, 2, 3, 4, 5]
        state = {"pb": 0, "epi": 0}

        def next_pb():
            state["pb"] = (state["pb"] + 1) % len(PBS)
            return PBS[state["pb"]]

        def mm(out, lhsT, rhs, start, stop, reads, writes, skip=False):
            A("pe", lambda e: e.matmul(out, lhsT=lhsT, rhs=rhs, start=start, stop=stop,
                                       skip_group_check=skip), reads, writes)

        def act(out, in_, func, reads, writes, bias=None, scale=None):
            kw = {}
            if bias is not None:
                kw["bias"] = bias
            if scale is not None:
                kw["scale"] = scale
            A("act", lambda e: e.activation(out=out, in_=in_, func=func, **kw), reads, writes)

        def ts(eng, out, in0, s1, s2, op0, op1, reads, writes):
            if op1 is None:
                A(eng, lambda e: e.tensor_scalar(out=out, in0=in0, scalar1=s1, scalar2=None, op0=op0),
                  reads, writes)
            else:
                A(eng, lambda e: e.tensor_scalar(out=out, in0=in0, scalar1=s1, scalar2=s2, op0=op0, op1=op1),
                  reads, writes)

        def tt(eng, out, in0, in1, op, reads, writes):
            A(eng, lambda e: e.tensor_tensor(out=out, in0=in0, in1=in1, op=op), reads, writes)

        def stt(out, in0, scalar, in1, op0, op1, reads, writes):
            A("dve", lambda e: e.scalar_tensor_tensor(out=out, in0=in0, scalar=scalar, in1=in1, op0=op0, op1=op1),
              reads, writes)

        def cp(eng, out, in_, reads, writes):
            A(eng, lambda e: e.tensor_copy(out=out, in_=in_), reads, writes)

        def mset(eng, ap, val, writes):
            A(eng, lambda e: e.memset(ap, val), (), writes)

        def dma(out, in_, reads, writes, q="sp", **kw):
            A(q, lambda e: e.dma_start(out=out, in_=in_, **kw), reads, writes, dma=True)

        def bslice(b):
            return slice(b * BLK, (b + 1) * BLK)

        dma(CBF[:], cbf_d, (), [("cbf",)], q="pool")
        dma(COLS[:], cols_d, (), [("cols",)])
        mset("dve", ONESB[:], 1.0, [("onesb",)])
        mset("dve", ONE1[:], 1.0, [("one1",)])
        ts("dve", NBF[:], COLS[:, CL["bf"]:CL["bf"] + nl], -1.0, None, ALU.mult, None, [("cols",)], [("nbf",)])
        act(CSIL[:], COLS[:, CL["cc"]:CL["cc"] + 8], AF.Exp, [("cols",)], [("csil",)], scale=-1.0)
        ts("dve", CSIL[:], CSIL[:], 1.0, None, ALU.add, None, [("csil",)], [("csil",)])
        A("dve", lambda e: e.reciprocal(out=CSIL[:], in_=CSIL[:]), [("csil",)], [("csil",)])
        tt("dve", CSIL[:], CSIL[:], COLS[:, CL["cc"]:CL["cc"] + 8], ALU.mult, [("csil",), ("cols",)], [("csil",)])

        ada_cnt = [0]

        def ada_unit(l, cb):
            wi = ada_cnt[0] % 2
            ada_cnt[0] += 1
            dma(WST[wi], wada_d[l, cb].rearrange("p (k c) -> p k c", c=512), (), [("wst", wi)])
            dma(T[0][0:1, :], bada_d[l, cb], (), [("T", 0)])
            for k in range(8):
                mm(PS[MB2][0:1, :], CSIL[:, k:k + 1], WST[wi][:, k, :], k == 0, False,
                   [("csil",), ("wst", wi)], [("ps", MB2)])
            mm(PS[MB2][0:1, :], ONE1[0:1, 0:1], T[0][0:1, :], False, True, [("one1",), ("T", 0)], [("ps", MB2)])
            act(T[1][0:1, :], PS[MB2][0:1, :], AF.Copy, [("ps", MB2)], [("T", 1)])
            pbk = next_pb()
            for j in range(4):
                mm(PS[pbk][:, j:j + 1], T[1][0:1, j * 128:(j + 1) * 128], ONE1[0:1, 0:1], True, True,
                   [("T", 1), ("one1",)], [("ps", pbk)], skip=True)
            act(MODT[:, l * 24 + cb * 4: l * 24 + cb * 4 + 4], PS[pbk][:, 0:4], AF.Copy,
                [("ps", pbk)], [("modT",)])

        def ada_finish(l):
            stt(GP1[:, l * 8:(l + 1) * 8], MODT[:, l * 24 + 8: l * 24 + 16], 1.0,
                COLS[:, CL["ng"] + l * 8: CL["ng"] + (l + 1) * 8], ALU.add, ALU.mult,
                [("modT",), ("cols",)], [("gp1",)])

        def shift_col(l, d):
            return MODT[:, l * 24 + d: l * 24 + d + 1]

        def gate_col(l, d):
            return MODT[:, l * 24 + 16 + d: l * 24 + 16 + d + 1]

        def gp1_col(l, d):
            return GP1[:, l * 8 + d: l * 8 + d + 1]

        for b in range(NB):
            PI_ = T[2][0:96, :].bitcast(I32)
            dma(PI_, pos_d[0:1, bslice(b)].broadcast_to([96, BLK]), (), [("T", 2)])
            cp("dve", T[3][0:96, :], PI_, [("T", 2)], [("T", 3)])
            ts("dve", T[3][0:96, :], T[3][0:96, :], col("invf", 0, 96), None, ALU.mult, None,
               [("T", 3), ("cols",)], [("T", 3)])
            ts("dve", T[4][0:96, :], T[3][0:96, :], float(1.0 / TWO_PI), None, ALU.mult, None,
               [("T", 3)], [("T", 4)])
            KI_ = T[2][0:96, :].bitcast(I32)
            cp("dve", KI_, T[4][0:96, :], [("T", 4)], [("T", 2)])
            cp("dve", T[4][0:96, :], KI_, [("T", 2)], [("T", 4)])
            stt(T[3][0:96, :], T[4][0:96, :], -C1, T[3][0:96, :], ALU.mult, ALU.add,
                [("T", 4), ("T", 3)], [("T", 3)])
            stt(T[3][0:96, :], T[4][0:96, :], -C2, T[3][0:96, :], ALU.mult, ALU.add,
                [("T", 4), ("T", 3)], [("T", 3)])

            def wrap(tr, tm):
                ts("dve", T[tm][0:96, :], T[tr][0:96, :], float(np.pi), -TWO_PI, ALU.is_gt, ALU.mult,
                   [("T", tr)], [("T", tm)])
                tt("dve", T[tr][0:96, :], T[tr][0:96, :], T[tm][0:96, :], ALU.add,
                   [("T", tr), ("T", tm)], [("T", tr)])
                ts("dve", T[tm][0:96, :], T[tr][0:96, :], -float(np.pi), TWO_PI, ALU.is_lt, ALU.mult,
                   [("T", tr)], [("T", tm)])
                tt("dve", T[tr][0:96, :], T[tr][0:96, :], T[tm][0:96, :], ALU.add,
                   [("T", tr), ("T", tm)], [("T", tr)])

            wrap(3, 4)
            ts("dve", T[5][0:96, :], T[3][0:96, :], float(np.pi / 2), None, ALU.add, None,
               [("T", 3)], [("T", 5)])
            wrap(5, 4)
            ts("dve", T[3][0:96, :], T[3][0:96, :], 3.141592, -3.141592, ALU.min, ALU.max, [("T", 3)], [("T", 3)])
            ts("dve", T[5][0:96, :], T[5][0:96, :], 3.141592, -3.141592, ALU.min, ALU.max, [("T", 5)], [("T", 5)])
            act(T[3][0:96, :], T[3][0:96, :], AF.Sin, [("T", 3)], [("T", 3)])
            act(T[5][0:96, :], T[5][0:96, :], AF.Sin, [("T", 5)], [("T", 5)])
            ts("dve", T[3][0:96, :], T[3][0:96, :], col("scl", 0, 96), None, ALU.mult, None,
               [("T", 3), ("cols",)], [("T", 3)])
            ts("dve", T[5][0:96, :], T[5][0:96, :], col("scl", 0, 96), None, ALU.mult, None,
               [("T", 5), ("cols",)], [("T", 5)])
            dma(rS_d[:, bslice(b)], T[3][0:96, :], [("T", 3)], [("ropeS", b)])
            dma(rC_d[:, bslice(b)], T[5][0:96, :], [("T", 5)], [("ropeC", b)])
            if b < 6:
                ada_unit(0, b)
        ada_finish(0)


        def end_phase(li, xsrc, src_name, after_block=None):
            final = (li == nl - 1)
            pend = []

            def slot(b, d):
                return (8 * b + d) % NXO

            def load_x(b, d):
                sl = slot(b, d)
                dma(XOS[sl], xsrc[d * 128:(d + 1) * 128, bslice(b)],
                    [(src_name, d, b)] if src_name else (), [("xo", sl)])

            def load_y(b):
                dma(YRB[b % 2], ysc_d[:, :, bslice(b)].rearrange("c p t -> p c t"),
                    [("ysc", c, b) for c in range(8)], [("yrb", b % 2)])

            if li >= 0:
                load_y(0)
            for d in range(8):
                load_x(0, d)
            for b in range(NB):
                yb = b % 2
                if b + 1 < NB:
                    if li >= 0:
                        load_y(b + 1)
                    for d in range(4):
                        load_x(b + 1, d)
                for d in range(8):
                    sl = slot(b, d)
                    xo = XOS[sl]
                    if li >= 0:
                        pbk = next_pb()
                        for c in range(8):
                            mm(PS[pbk][:], WOUT[:, c, d * 128:(d + 1) * 128], YRB[yb][:, c, :], c == 0, c == 7,
                               [("wout",), ("yrb", yb)], [("ps", pbk)])
                        stt(xo, PS[pbk][:], gate_col(li, d), xo, ALU.mult, ALU.add,
                            [("ps", pbk), ("modT",), ("xo", sl)], [("xo", sl)])
                        if not final:
                            dma(xs_d[d * 128:(d + 1) * 128, bslice(b)], xo, [("xo", sl)], [("xs", d, b)])
                    if final and not last:
                        dma(out_d[d * 128:(d + 1) * 128, bslice(b)], xo, [("xo", sl)], [("out", d, b)])
                    if not (final and not last):
                        si = d % 2
                        act(SQ[si][:], xo, AF.Square, [("xo", sl)], [("sq", si)])
                        pend.append((d, si))
                        if len(pend) > 1:
                            dd, ss_ = pend.pop(0)
                            mm(PS[MB][:], ONESB[:], SQ[ss_][:], dd == 0, dd == 7, [("onesb",), ("sq", ss_)],
                               [("ps", MB)])
                if b in (0, 1):
                    flush_tail()
                if final and not last:
                    if b + 1 < NB:
                        for d in range(4, 8):
                            load_x(b + 1, d)
                    continue
                while pend:
                    dd, ss_ = pend.pop(0)
                    mm(PS[MB][:], ONESB[:], SQ[ss_][:], dd == 0, dd == 7, [("onesb",), ("sq", ss_)], [("ps", MB)])
                act(T[6][:], PS[MB][:], AF.Ln, [("ps", MB)], [("T", 6)], bias=EPS, scale=1.0 / D)
                act(T[6][:], T[6][:], AF.Exp, [("T", 6)], [("T", 6)], scale=-0.5)
                for d in range(8):
                    sl = slot(b, d)
                    xo = XOS[sl]
                    if not final:
                        tt("dve", xo, xo, T[6][:], ALU.mult, [("xo", sl), ("T", 6)], [("xo", sl)])
                        act(hT[:, d, bslice(b)], xo, AF.Identity, [("xo", sl), ("gp1",), ("modT",)],
                            [("hT", d, b)], bias=shift_col(li + 1, d), scale=gp1_col(li + 1, d))
                    else:
                        stt(xo, xo, col("fg", d), T[6][:], ALU.mult, ALU.mult,
                            [("xo", sl), ("cols",), ("T", 6)], [("xo", sl)])
                        dma(out_d[d * 128:(d + 1) * 128, bslice(b)], xo, [("xo", sl)], [("out", d, b)])
                    if d < 4 and b + 1 < NB:
                        load_x(b + 1, d + 4)
                if after_block is not None:
                    after_block(b)

        def attention_pair(kd, mask, chunk, wg, prefetch=None, extra=None):
            pus = [(I, kb) for I in range(NB) for kb in range(4 * (I + 1))]
            n = len(pus)
            deferred = []

            def obank(I, hd):
                return 4 + (2 * I + hd) % 3

            def gate_proj(I, pu):
                gb = I % 2
                for k in range(8):
                    mm(PS[GB][:], wg[:, k, :], hT[:, k, bslice(I)], k == 0, k == 7,
                       [("WP",), ("hT", k, I)], [("ps", GB)])
                cp("dve", GG[gb][:], PS[GB][:], [("ps", GB)], [("G", gb)])

                def later():
                    act(EE[gb][:], PS[GB][:], AF.Exp, [("ps", GB)], [("E", gb)] + ([("ps", GB)] if _ordrr else []),
                        scale=-1.0)
                deferred.append((pu + 1, later))

            def emit_S(pu):
                I, kb = pus[pu]
                if kb == 0:
                    gate_proj(I, pu)
                    if I == NB - 1 and prefetch is not None and not _nopf:
                        prefetch()
                jd = kb - 4 * I
                qlo = 128 * max(0, jd)
                j = pu % 2
                for hd in (0, 1):
                    bk = 2 * j + hd
                    mm(PS[bk][:, qlo:512], KT[hd][0:kd[hd], kb * 128:(kb + 1) * 128],
                       QT[hd][0:kd[hd], I * BLK + qlo:(I + 1) * BLK], True, jd < 0,
                       [("KT", hd, kb // 4), ("QT", hd, I)], [("ps", bk)])
                    if jd >= 0:
                        mm(PS[bk][:, qlo:qlo + 128], ident, mask, False, True, [("cbf",)], [("ps", bk)])

            def epilogue(I, pu):
                eb = I % 2
                oa, ob = obank(I, 0), obank(I, 1)
                cp("dve", OT[eb][0][:], PS[oa][:], [("ps", oa)], [("Ot", eb, 0)])
                cp("dve", OT[eb][1][:], PS[ob][:], [("ps", ob)], [("Ot", eb, 1)])
                dma(LM[eb][0:64, :], OT[eb][0][64:128, :], [("Ot", eb, 0)], [("Lm", eb)])
                dma(LM[eb][64:128, :], OT[eb][1][0:64, :], [("Ot", eb, 1)], [("Lm", eb)])

                def later():
                    stt(LM[eb][:], EE[eb][:], 1.0, LM[eb][:], ALU.add, ALU.mult,
                        [("E", eb), ("Lm", eb)], [("Lm", eb)])
                    if I == NB - 1:
                        act(LM[eb][:], LM[eb][:], AF.Ln, [("Lm", eb)], [("Lm", eb)])
                        act(LM[eb][:], LM[eb][:], AF.Exp, [("Lm", eb)], [("Lm", eb)], scale=-1.0)
                        tail_next.append(later2)
                        return
                    A("dve", lambda e: e.reciprocal(out=LM[eb][:], in_=LM[eb][:]), [("Lm", eb)], [("Lm", eb)])
                    later2()

                def later2():
                    em = "dve" if I == NB - 1 else "pool"
                    tt(em, LM[eb][:], LM[eb][:], GG[eb][:], ALU.mult, [("Lm", eb), ("G", eb)], [("Lm", eb)])
                    tt(em, YS[eb][0:64, :], OT[eb][0][0:64, :], LM[eb][0:64, :], ALU.mult,
                       [("Ot", eb, 0), ("Lm", eb)], [("ys", eb)])
                    tt(em, YS[eb][64:128, :], OT[eb][1][64:128, :], LM[eb][64:128, :], ALU.mult,
                       [("Ot", eb, 1), ("Lm", eb)], [("ys", eb)])
                    dma(ysc_d[chunk, :, bslice(I)], YS[eb][:], [("ys", eb)], [("ysc", chunk, I)])
                deferred.append((pu + 4, later))

            def emit_rest(pu):
                I, kb = pus[pu]
                jd = kb - 4 * I
                qlo = 128 * max(0, jd)
                N = 512 - qlo
                j = pu % 2
                pt = pu % 3
                for hd in (0, 1):
                    act(PT[pt][:, hd * 512: hd * 512 + N], PS[2 * j + hd][:, qlo:512], AF.Exp,
                        [("ps", 2 * j + hd)], [("PT", pt, hd)])
                lastkb = (kb == 4 * (I + 1) - 1)
                for hd in (0, 1):
                    ok = obank(I, hd)
                    lhsT = VV[:, kb, 0:128] if hd == 0 else VV[:, kb, 64:192]
                    mm(PS[ok][:, qlo:512], lhsT, PT[pt][:, hd * 512: hd * 512 + N], kb == 0, lastkb,
                       [("PT", pt, hd), ("VV", kb // 4)], [("ps", ok)])
                if lastkb:
                    epilogue(I, pu)

            emit_S(0)
            for pu in range(n):
                if pu + 1 < n:
                    emit_S(pu + 1)
                if extra is not None:
                    extra(pu, deferred)
                emit_rest(pu)
                keep = []
                for due, fn in deferred:
                    if due <= pu:
                        fn()
                    else:
                        keep.append((due, fn))
                deferred[:] = keep
            tail_fns.extend(fn for due, fn in deferred)
            if prefetch is not None and _nopf:
                prefetch()

        tail_fns = []
        tail_next = []

        def flush_tail():
            fns = list(tail_fns)
            del tail_fns[:]
            for fn in fns:
                fn()
            tail_fns.extend(tail_next)
            del tail_next[:]

        def v_evac(pbk, g):
            pv = PS[pbk][:].rearrange("p (j c) -> p j c", c=128)
            cp("dve", VV[:, 4 * g:4 * g + 4, 0:64], pv[:, :, 0:64], [("ps", pbk)], [("VV", g)])
            act(VV[:, 4 * g:4 * g + 4, 128:192], pv[:, :, 64:128], AF.Copy, [("ps", pbk)],
                [("VV", g)] + ([("ps", pbk)] if _ordrr else []))

        try:
          ada_units = [(l, cb) for l in range(1, nl) for cb in range(6)]
          per_blk = (len(ada_units) + NB - 1) // NB

          def after_block(b):
              for _ in range(per_blk):
                  if ada_units:
                      l_, cb_ = ada_units.pop(0)
                      ada_unit(l_, cb_)
                      if cb_ == 5:
                          ada_finish(l_)

          end_phase(-1, xT_in, None, after_block)
          while ada_units:
              after_block(0)
          sch.alias(["wst"], ["qn", "kvn", "kr"])
          stop_at("pro")
          for l in range(nl):
              sch.alias(["xo", "yrb"], ["QT", "KT", "VV"])
              mset("pool", VV[:, :, 64:128], 1.0, [("VV", g) for g in range(8)])
              WC_ql = WC[:, 0:2048].rearrange("p (k c) -> p k c", c=256)
              WC_kvl = WC[:, 2048:3072].rearrange("p (k c) -> p k c", c=128)
              WC_kr = WC[:, 3072:3328].rearrange("p (k c) -> p k c", c=32)
              WC_krr = WC[:, 3328:3584].rearrange("p (k c) -> p k c", c=32)
              WC_ff = WC[:, 3584:4160].rearrange("p (k c) -> p k c", c=72)

              def load_wc(ll, WC_krr=WC_krr):
                  dma(WC[:], wcom_d[ll], (), [("WC",)], q="pool", max_dma_last_dim=4096)
                  ts("pool", WC_krr[:, :, 0:16], WC_krr[:, :, 0:16], -1.0, None, ALU.mult, None,
                     [("WC",)], [("WC",)])

              if l == 0:
                  load_wc(0)
              for b in range(NB):
                  dma(RC[0:96, :], rC_d[:, bslice(b)], [("ropeC", b)], [("rc",)])
                  dma(RS[0:96, :], rS_d[:, bslice(b)], [("ropeS", b)], [("rsn",)])
                  pq = []
                  for c in range(2):
                      pbk = next_pb()
                      pq.append(pbk)
                      for k in range(8):
                          mm(PS[pbk][:], WC_ql[:, k, c * 128:(c + 1) * 128], hT[:, k, bslice(b)], k == 0, k == 7,
                             [("WC",), ("hT", k, b)], [("ps", pbk)])
                      cp("dve", T[c][:], PS[pbk][:], [("ps", pbk)], [("T", c)])
                      act(SQ[c][:], T[c][:], AF.Square, [("T", c)], [("sq", c)])
                  pkv = next_pb()
                  for k in range(8):
                      mm(PS[pkv][:], WC_kvl[:, k, :], hT[:, k, bslice(b)], k == 0, k == 7,
                         [("WC",), ("hT", k, b)], [("ps", pkv)])
                  cp("dve", T[2][:], PS[pkv][:], [("ps", pkv)], [("T", 2)])
                  p1 = next_pb()
                  for k in range(8):
                      mm(PS[p1][0:32, :], WC_kr[:, k, :], hT[:, k, bslice(b)], k == 0, k == 7,
                         [("WC",), ("hT", k, b)], [("ps", p1)])
                  tA, kA = (T[3], ("T", 3)) if b % 2 == 0 else (OT[0][0], ("Ot", 0, 0))
                  tB, kB = (T[4], ("T", 4)) if b % 2 == 0 else (OT[0][1], ("Ot", 0, 1))
                  tt("dve", tA[0:32, :], PS[p1][0:32, :], RC[0:32, :], ALU.mult, [("ps", p1), ("rc",)], [kA])
                  p2 = next_pb()
                  for k in range(8):
                      mm(PS[p2][0:32, :], WC_krr[:, k, :], hT[:, k, bslice(b)], k == 0, k == 7,
                         [("WC",), ("hT", k, b)], [("ps", p2)])
                  tt("dve", tB[0:32, :], PS[p2][0:32, :], RS[0:32, :], ALU.mult, [("ps", p2), ("rsn",)], [kB])
                  tt("pool", KR[0:32, bslice(b)], tA[0:32, :], tB[0:32, :], ALU.add,
                     [kA, kB], [("kr", b)])
                  for c in range(2):
                      mm(PS[MB][:], ONESB[:], SQ[c][:], c == 0, c == 1, [("onesb",), ("sq", c)], [("ps", MB)])
                  act(T[6][:], PS[MB][:], AF.Ln, [("ps", MB)], [("T", 6)], bias=EPS, scale=1.0 / 256)
                  act(T[6][:], T[6][:], AF.Exp, [("T", 6)], [("T", 6)], scale=-0.5)
                  for c in range(2):
                      stt(QN[:, c, bslice(b)], T[c][:], col("qg", l * 2 + c), T[6][:], ALU.mult, ALU.mult,
                          [("T", c), ("cols",), ("T", 6)], [("qn", b)])
                  act(SQ[0][:], T[2][:], AF.Square, [("T", 2)], [("sq", 0)])
                  mm(PS[MB2][:], ONESB[:], SQ[0][:], True, True, [("onesb",), ("sq", 0)], [("ps", MB2)])
                  act(T[5][:], PS[MB2][:], AF.Ln, [("ps", MB2)], [("T", 5)], bias=EPS, scale=1.0 / 128)
                  act(T[5][:], T[5][:], AF.Exp, [("T", 5)], [("T", 5)], scale=-0.5)
                  stt(KVN[:, bslice(b)], T[2][:], col("kvg", l), T[5][:], ALU.mult, ALU.mult,
                      [("T", 2), ("cols",), ("T", 5)], [("kvn", b)])

              stop_at("mlacom")
              def load_mla(p, l=l):
                  dma(WP[:, 0:W_MLA], wmla_d[l, p], (), [("WP",)], q="pool", max_dma_last_dim=4096)

              def load_fox(p, l=l):
                  dma(WP[:, 0:W_FOX], wfox_d[l, p], (), [("WP",)], q="pool", max_dma_last_dim=4096)

              load_mla(0)
              for p in range(4):
                  uq = [WP[:, h * 384: h * 384 + 192].rearrange("p (k c) -> p k c", c=96) for h in range(2)]
                  uqr = [WP[:, h * 384 + 192: h * 384 + 384].rearrange("p (k c) -> p k c", c=96) for h in range(2)]
                  uk = [WP[:, 768 + h * 96: 768 + (h + 1) * 96] for h in range(2)]
                  uv = WP[:, 960:1088]
                  wg = WP[:, 1088:2112].rearrange("p (k c) -> p k c", c=128)
                  for h in range(2):
                      act(uqr[h][:, :, 64:80], uqr[h][:, :, 64:80], AF.Copy, [("WP",)], [("WP",)], scale=-1.0)
                  for b in range(NB):
                      dma(RC[0:96, :], rC_d[:, bslice(b)], [("ropeC", b)], [("rc",)])
                      dma(RS[0:96, :], rS_d[:, bslice(b)], [("ropeS", b)], [("rsn",)])
                      for h in range(2):
                          p1 = next_pb()
                          for k in range(2):
                              mm(PS[p1][0:96, :], uq[h][:, k, :], QN[:, k, bslice(b)], k == 0, k == 1,
                                 [("WP",), ("qn", b)], [("ps", p1)])
                          tA, kA = (T[3], ("T", 3)) if h == 0 else (OT[0][0], ("Ot", 0, 0))
                          tB, kB = (T[4], ("T", 4)) if h == 0 else (OT[0][1], ("Ot", 0, 1))
                          tt("dve", tA[64:96, :], PS[p1][64:96, :], RC[64:96, :], ALU.mult,
                             [("ps", p1), ("rc",)], [kA])
                          act(QT[h][0:64, bslice(b)], PS[p1][0:64, :], AF.Copy, [("ps", p1)],
                              [("QT", h, b), ("ps", p1)], scale=MLA_SCALE)
                          p2 = next_pb()
                          for k in range(2):
                              mm(PS[p2][0:96, :], uqr[h][:, k, :], QN[:, k, bslice(b)], k == 0, k == 1,
                                 [("WP",), ("qn", b)], [("ps", p2)])
                          tt("dve", tB[64:96, :], PS[p2][64:96, :], RS[64:96, :], ALU.mult,
                             [("ps", p2), ("rsn",)], [kB])
                          tt("pool", QT[h][64:96, bslice(b)], tA[64:96, :], tB[64:96, :], ALU.add,
                             [kA, kB], [("QT", h, b)])
                          pk = next_pb()
                          mm(PS[pk][0:96, :], uk[h], KVN[:, bslice(b)], True, False,
                             [("WP",), ("kvn", b)], [("ps", pk)])
                          mm(PS[pk][0:96, :], selr, KR[0:32, bslice(b)], False, True,
                             [("cbf",), ("kr", b)], [("ps", pk)])
                          act(KT[h][0:96, bslice(b)], PS[pk][0:96, :], AF.Copy, [("ps", pk)], [("KT", h, b)])
                      pv = next_pb()
                      for j in range(4):
                          t = 4 * b + j
                          mm(PS[pv][:, j * 128:(j + 1) * 128], KVN[:, t * 128:(t + 1) * 128], uv, True, True,
                             [("WP",), ("kvn", b)], [("ps", pv)], skip=True)
                      v_evac(pv, b)
                      if b in (1, 2):
                          flush_tail()
                  stop_at("mlaproj")
                  extra = None
                  if p == 3:
                      sch.alias(["kvn"], ["C"])
                      sch.alias(["qn"], ["wout"])
                      dma(WOUT, wout_d[l].rearrange("p (k c) -> p k c", c=1024), (), [("wout",)], q="pool",
                          max_dma_last_dim=4096)
                      mset("pool", CC[0:32, :], 1.0, [("C", b) for b in range(NB)])
                      mset("dve", T[5][:], 1.0, [("T", 5)])

                      def fox_common(b, deferred, pu, l=l, WC_ff=WC_ff):
                          for k in range(8):
                              mm(PS[GB][0:72, :], WC_ff[:, k, :], hT[:, k, bslice(b)], k == 0, k == 7,
                                 [("WC",), ("hT", k, b)], [("ps", GB)])

                          def later():
                              act(T[0][0:72, :], PS[GB][0:72, :], AF.Exp, [("ps", GB), ("nbf",)],
                                  [("T", 0), ("ps", GB)], bias=NBF[0:72, l:l + 1], scale=-1.0)
                              act(T[0][0:72, :], T[0][0:72, :], AF.Ln, [("T", 0)], [("T", 0)], bias=1.0, scale=1.0)
                              cur, prv = T[1 + b % 2], T[1 + (b + 1) % 2]
                              kc, kp = ("T", 1 + b % 2), ("T", 1 + (b + 1) % 2)
                              init = 0.0 if b == 0 else prv[0:72, 511:512]
                              A("dve", lambda e, cur=cur, init=init: e.tensor_tensor_scan(
                                  out=cur[0:72, :], data0=T[5][0:72, :], data1=T[0][0:72, :], initial=init,
                                  op0=ALU.mult, op1=ALU.add), [("T", 5), ("T", 0), kp], [kc])
                              cp("pool", HB[0][0:72, :], cur[0:72, :], [kc], [("sq", 0)])
                              cp("pool", CC[0:8, bslice(b)], HB[0][0:8, :], [("sq", 0)], [("C", b)])
                              tt("pool", T[3][0:72, :], cur[0:72, :], HB[0][0:72, :], ALU.subtract,
                                 [kc, ("sq", 0)], [("T", 3)])
                              cp("pool", HB[1][0:72, :], T[3][0:72, :], [("T", 3)], [("sq", 1)])
                              cp("pool", CC[32:40, bslice(b)], HB[1][32:40, :], [("sq", 1)], [("C", b)])
                              tt("pool", T[4][0:72, :], T[3][0:72, :], HB[1][0:72, :], ALU.subtract,
                                 [("T", 3), ("sq", 1)], [("T", 4)])
                              cp("pool", CC[64:72, bslice(b)], T[4][64:72, :], [("T", 4)], [("C", b)])
                          deferred.append((pu + 1, later))

                      fc_sched = {6: 0, 14: 1, 26: 2, 42: 3, 62: 4, 86: 5, 114: 6, 122: 7}

                      def extra(pu, deferred):
                          if pu in fc_sched:
                              fox_common(fc_sched[pu], deferred, pu)
                  attention_pair((96, 96), maskM, 4 + p, wg,
                                 prefetch=(lambda p=p: load_mla(p + 1)) if p < 3 else (lambda: load_fox(0)),
                                 extra=extra)
                  if p == 3 and l + 1 < nl:
                      load_wc(l + 1)

              stop_at("mla")
              stop_at("foxcom")
              for p in range(4):
                  wq = WP[:, 0:1024].rearrange("p (k c) -> p k c", c=128)
                  wk = WP[:, 1024:2048].rearrange("p (k c) -> p k c", c=128)
                  wv = WP[:, 2048:3072].rearrange("p (k c) -> p k c", c=128)
                  wg = WP[:, 3072:4096].rearrange("p (k c) -> p k c", c=128)
                  for b in range(NB):
                      if p == 0:
                          mset("pool", QT[1][0:64, bslice(b)], 0.0, [("QT", 1, b)])
                          mset("pool", KT[1][0:64, bslice(b)], 0.0, [("KT", 1, b)])
                      for (wt, sel, dst, nm, sc) in ((wq, selq[p], QT, "QT", 0.125), (wk, selk[p], KT, "KT", None)):
                          pm = next_pb()
                          for k in range(8):
                              mm(PS[pm][:], wt[:, k, :], hT[:, k, bslice(b)], k == 0, k == 7,
                                 [("WP",), ("hT", k, b)], [("ps", pm)])
                          pa = next_pb()
                          mm(PS[pa][0:70, :], sel, CC[0:72, bslice(b)], True, True,
                             [("cbf",), ("C", b)], [("ps", pa)])
                          if sc is not None:
                              ts("dve", dst[0][0:64, bslice(b)], PS[pm][0:64, :], sc, None, ALU.mult, None,
                                 [("ps", pm)], [(nm, 0, b)])
                              ts("dve", dst[1][64:128, bslice(b)], PS[pm][64:128, :], sc, None, ALU.mult, None,
                                 [("ps", pm)], [(nm, 1, b)])
                              ts("dve", dst[0][64:70, bslice(b)], PS[pa][64:70, :], sc, None, ALU.mult, None,
                                 [("ps", pa)], [(nm, 0, b)])
                              ts("dve", dst[1][0:6, bslice(b)], PS[pa][0:6, :], sc, None, ALU.mult, None,
                                 [("ps", pa)], [(nm, 1, b)])
                          else:
                              act(dst[0][0:64, bslice(b)], PS[pm][0:64, :], AF.Copy, [("ps", pm)], [(nm, 0, b)])
                              act(dst[1][64:128, bslice(b)], PS[pm][64:128, :], AF.Copy, [("ps", pm)], [(nm, 1, b)])
                              act(dst[0][64:70, bslice(b)], PS[pa][64:70, :], AF.Copy, [("ps", pa)], [(nm, 0, b)])
                              act(dst[1][0:6, bslice(b)], PS[pa][0:6, :], AF.Copy, [("ps", pa)], [(nm, 1, b)])
                      pv = next_pb()
                      for j in range(4):
                          t = 4 * b + j
                          for k in range(8):
                              mm(PS[pv][:, j * 128:(j + 1) * 128], hT[:, k, t * 128:(t + 1) * 128], wv[:, k, :],
                                 k == 0, k == 7, [("WP",), ("hT", k, b)], [("ps", pv)], skip=True)
                      v_evac(pv, b)
                      if b in (1, 2):
                          flush_tail()
                  stop_at(f"foxproj{p}")
                  attention_pair((70, 128), maskF, p, wg, prefetch=(lambda p=p: load_fox(p + 1)) if p < 3 else None)
                  stop_at(f"foxatt{p}")

              stop_at("fox")
              sch.alias(["QT", "KT", "VV"], ["xo", "yrb"])
              end_phase(l, xT_in if l == 0 else xs_d, None if l == 0 else "xs")
              sch.alias(["C"], ["kvn"])
              sch.alias(["wout"], ["qn"])

        except _Stop:
            pass

        A("sp", None, [("out", d, b) for d in range(8) for b in range(NB)], ())

        sem_eng = {e: es.enter_context(nc.semaphore(f"s_{e}")) for e in ("pe", "act", "dve", "pool")}
        dma_sems = {
            "sp": [es.enter_context(nc.semaphore(f"d_sp{i}")) for i in range(16)],
            "pool": [es.enter_context(nc.semaphore(f"d_pl{i}")) for i in range(6)],
        }
        sch.finalize(sem_eng, dma_sems)
        block = es.enter_context(nc.Block())

        @block.tensor
        def _(e):
            sch.emit("pe", e)

        @block.scalar
        def _(e):
            sch.emit("act", e)

        @block.vector
        def _(e):
            sch.emit("dve", e)

        @block.gpsimd
        def _(e):
            sch.emit("pool", e)

        @block.sync
        def _(e):
            sch.emit("sp", e)

    return nc


def _consts_bf():
    cb = np.zeros((128, CB_N), np.float32)
    cb[:, CB_ID:CB_ID + 128] = np.eye(128, dtype=np.float32)
    k = np.arange(128)[:, None]
    q = np.arange(128)[None, :]
    cb[:, CB_MF:CB_MF + 128] = np.where(k <= q, 0.0, -30000.0)
    cb[:, CB_MM:CB_MM + 128] = np.where((k // 64) <= (q // 64), 0.0, -30000.0)
    for p in range(4):
        ha, hb = 2 * p, 2 * p + 1
        sq = np.zeros((128, 70), np.float32)
        sk = np.zeros((128, 70), np.float32)
        for j, g in enumerate((0, 32, 64)):
            sq[g + ha, 64 + j] = -8.0
            sq[g + hb, 0 + j] = -8.0
            sk[g + ha, 67 + j] = 1.0
            sk[g + hb, 3 + j] = 1.0
        sq[8, 67:70] = 8.0
        sq[8, 3:6] = 8.0
        sk[8, 64:67] = 1.0
        sk[8, 0:3] = 1.0
        cb[:, CB_SQ + p * 70: CB_SQ + (p + 1) * 70] = sq
        cb[:, CB_SK + p * 70: CB_SK + (p + 1) * 70] = sk
    sr = np.zeros((128, 96), np.float32)
    for r in range(32):
        sr[r, 64 + r] = 1.0
    cb[:, CB_SR:CB_SR + 96] = sr
    return cb


def _kchunks(w):
    kk = w.shape[0] // 128
    return w.reshape(kk, 128, w.shape[1]).transpose(1, 0, 2)


def _pack_weights(layers, w_in, w_uq, w_ukv, w_out, w_ada, b_ada):
    nl = len(layers)
    wfox = np.zeros((nl, 4, 128, W_FOX), np.float32)
    wmla = np.zeros((nl, 4, 128, W_MLA), np.float32)
    wcom = np.zeros((nl, 128, W_COM), np.float32)
    wout = np.zeros((nl, 128, 8192), np.float32)
    wada = np.zeros((nl, 6, 128, 4096), np.float32)
    bada = np.zeros((nl, 6, 1, 512), np.float32)
    for i, l in enumerate(layers):
        wi = _kchunks(w_in[l])
        uqc = _kchunks(w_uq[l])
        ukv = w_ukv[l]
        for p in range(4):
            buf = np.zeros((128, W_FOX), np.float32)
            buf[:, 0:1024] = wi[:, :, p * 128:(p + 1) * 128].reshape(128, 1024)
            buf[:, 1024:2048] = wi[:, :, 512 + p * 128: 512 + (p + 1) * 128].reshape(128, 1024)
            buf[:, 2048:3072] = wi[:, :, 1024 + p * 128: 1024 + (p + 1) * 128].reshape(128, 1024)
            buf[:, 3072:4096] = wi[:, :, 1544 + p * 128: 1544 + (p + 1) * 128].reshape(128, 1024)
            wfox[i, p] = buf
            buf = np.zeros((128, W_MLA), np.float32)
            for h in range(2):
                hd = 2 * p + h
                buf[:, h * 384: h * 384 + 192] = uqc[:, :, hd * 96:(hd + 1) * 96].reshape(128, 192)
                t = np.zeros((128, 2, 96), np.float32)
                t[:, :, 64:80] = uqc[:, :, hd * 96 + 80: hd * 96 + 96]
                t[:, :, 80:96] = uqc[:, :, hd * 96 + 64: hd * 96 + 80]
                buf[:, h * 384 + 192: h * 384 + 384] = t.reshape(128, 192)
                buf[:, 768 + h * 96: 768 + h * 96 + 64] = ukv[:, hd * 128: hd * 128 + 64]
                buf[:, 960 + h * 64: 960 + (h + 1) * 64] = ukv[:, hd * 128 + 64: hd * 128 + 128]
            buf[:, 1088:2112] = wi[:, :, 2472 + p * 128: 2472 + (p + 1) * 128].reshape(128, 1024)
            wmla[i, p] = buf
        buf = np.zeros((128, W_COM), np.float32)
        buf[:, 0:2048] = wi[:, :, 2056:2312].reshape(128, 2048)
        buf[:, 2048:3072] = wi[:, :, 2312:2440].reshape(128, 1024)
        buf[:, 3072:3328] = wi[:, :, 2440:2472].reshape(128, 256)
        t = np.zeros((128, 8, 32), np.float32)
        t[:, :, 0:16] = wi[:, :, 2456:2472]
        t[:, :, 16:32] = wi[:, :, 2440:2456]
        buf[:, 3328:3584] = t.reshape(128, 256)
        t = np.zeros((128, 8, 72), np.float32)
        for g in (0, 32, 64):
            t[:, :, g:g + 8] = wi[:, :, 1536:1544]
        buf[:, 3584:4160] = t.reshape(128, 576)
        wcom[i] = buf
        wout[i] = _kchunks(w_out[l]).reshape(128, 8192)
        wa = _kchunks(w_ada[l])
        for cb in range(6):
            wada[i, cb] = wa[:, :, cb * 512:(cb + 1) * 512].reshape(128, 4096)
            bada[i, cb, 0] = b_ada[l][cb * 512:(cb + 1) * 512]
    return wfox, wmla, wcom, wout, wada, bada


def _pack_cols(layers, b, c, norm_g, final_g, q_norm_g, kv_norm_g, b_f):
    nl = len(layers)
    CL = _col_layout(nl)
    cols = np.zeros((128, CL["_n"]), np.float32)
    for i, l in enumerate(layers):
        cols[:, CL["ng"] + i * 8: CL["ng"] + (i + 1) * 8] = norm_g[l].reshape(8, 128).T
        cols[:, CL["qg"] + i * 2: CL["qg"] + (i + 1) * 2] = q_norm_g[l].reshape(2, 128).T
        cols[:, CL["kvg"] + i] = kv_norm_g[l]
        for g in (0, 32, 64):
            cols[g:g + 8, CL["bf"] + i] = b_f[l]
    cols[:, CL["fg"]:CL["fg"] + 8] = final_g.reshape(8, 128).T
    inv_freq = (1.0 / (10000.0 ** (np.arange(0, 32, 2, dtype=np.float32) / np.float32(32)))).astype(np.float32)
    for base in (0, 64):
        cols[base:base + 16, CL["invf"]] = inv_freq
        cols[base + 16:base + 32, CL["invf"]] = inv_freq
    cols[0:64, CL["scl"]] = 1.0
    cols[64:128, CL["scl"]] = MLA_SCALE
    cols[:, CL["cc"]:CL["cc"] + 8] = c[b].reshape(8, 128).T
    return cols


_PROG_CACHE = {}


def _program(nl, last):
    key = (nl, last)
    if key not in _PROG_CACHE:
        _PROG_CACHE[key] = build_program(nl, last)
    return _PROG_CACHE[key]


def _launch(layers, last, xT_list, inputs, cbf):
    nl = len(layers)
    wfox, wmla, wcom, wout, wada, bada = _pack_weights(
        layers, inputs["w_in"], inputs["w_uq"], inputs["w_ukv"], inputs["w_out"], inputs["w_ada"],
        inputs["b_ada"])
    in_maps = []
    for b in range(8):
        in_maps.append({
            "xT": xT_list[b],
            "pos": np.ascontiguousarray(inputs["positions"][b].reshape(1, S)).astype(np.int32),
            "cols": _pack_cols(layers, b, inputs["c"], inputs["norm_g"], inputs["final_g"],
                               inputs["q_norm_g"], inputs["kv_norm_g"], inputs["b_f"]),
            "cbf": cbf, "wfox": wfox, "wmla": wmla, "wcom": wcom, "wout": wout, "wada": wada, "bada": bada,
        })
    nc = _program(nl, last)
    res = run_bass_kernel_spmd(nc, in_maps, core_ids=list(range(8)))
    return [np.asarray(r["out"]) for r in res.results]


LAYER_GROUPS = [[0, 1, 2, 3]]


def kernel(**inputs):
    inputs = {k: np.asarray(v) for k, v in inputs.items()}
    x = inputs["x"].astype(np.float32, copy=False)
    cbf = _consts_bf()
    xT = [np.ascontiguousarray(x[b].T) for b in range(8)]
    for gi, layers in enumerate(LAYER_GROUPS):
        xT = _launch(layers, gi == len(LAYER_GROUPS) - 1, xT, inputs, cbf)
    out = np.stack([np.ascontiguousarray(xT[b].T) for b in range(8)], axis=0)
    return out.astype(np.float32, copy=False)
```
